# Optimizing a Trainium2 kernel written in Bass

```python
import math
import jax, jax.numpy as jnp
from jax import lax
import numpy as np

D_MODEL = 1024
BATCH = 16
SEQ = 2048
DEPTH = 1

MEM_LEN = 256
EPS = 1e-6
GMLP_GROUPS = 6
GMLP_GROUP_DIM = 128
GMLP_WIDTH = GMLP_GROUPS * GMLP_GROUP_DIM
CHUNK = 128
MOBA_HEADS = 12
HEAD_DIM = 64
MOBA_WIDTH = MOBA_HEADS * HEAD_DIM
MOBA_BLOCK = 256
MOBA_TOPK = 3
Q_CHUNK = 64
REL_BUCKETS = 32
REL_MAX_DIST = 128
MEM_HEADS = 4
MEM_HEAD_DIM = 128
MEM_WIDTH = MEM_HEADS * MEM_HEAD_DIM
N_BRANCHES = 3
D_FF = -(-8 * D_MODEL // (3 * 256)) * 256
IN_WIDTH = 2 * GMLP_WIDTH + 3 * MOBA_WIDTH + MEM_WIDTH + N_BRANCHES * D_MODEL
NEG = -1e30

kernel_name = "hybrid_gmlp_moba_memxattn_block"


def rms_norm(x, g):
    xf = x.astype(jnp.float32)
    y = xf * lax.rsqrt(jnp.mean(xf * xf, axis=-1, keepdims=True) + EPS)
    return (y * g.astype(jnp.float32)).astype(x.dtype)


def layer_norm(x, g, b):
    xf = x.astype(jnp.float32)
    mu = jnp.mean(xf, axis=-1, keepdims=True)
    xc = xf - mu
    y = xc * lax.rsqrt(jnp.mean(xc * xc, axis=-1, keepdims=True) + EPS)
    return (y * g.astype(jnp.float32) + b.astype(jnp.float32)).astype(x.dtype)


def t5_bucket(dist):
    max_exact = REL_BUCKETS // 2
    n = jnp.maximum(dist, 0)
    nf = jnp.maximum(n, 1).astype(jnp.float32)
    large = max_exact + (jnp.log(nf / max_exact) / math.log(REL_MAX_DIST / max_exact)
                         * (REL_BUCKETS - max_exact)).astype(jnp.int32)
    large = jnp.minimum(large, REL_BUCKETS - 1)
    return jnp.where(n < max_exact, n, large)


def gmlp_branch(u, v, ln_g, ln_b, w_s, b_s):
    B, S, _ = v.shape
    u = jax.nn.gelu(u)
    v = layer_norm(jax.nn.gelu(v), ln_g, ln_b)
    vc = v.reshape(B, S // CHUNK, CHUNK, GMLP_GROUPS, GMLP_GROUP_DIM)
    causal = jnp.tril(jnp.ones((CHUNK, CHUNK), dtype=bool))
    ws = jnp.where(causal[None], w_s, jnp.zeros_like(w_s))
    mixed = jnp.einsum('gts,bcsgd->bctgd', ws, vc) + b_s.T[None, None, :, :, None]
    return u * mixed.reshape(B, S, GMLP_WIDTH)


def moba_attention(q, k, v, rel_bias):
    B, H, S, d = q.shape
    n_blk = -(-S // MOBA_BLOCK)
    pad = n_blk * MOBA_BLOCK - S
    kp = jnp.pad(k, ((0, 0), (0, 0), (0, pad), (0, 0)))
    vp = jnp.pad(v, ((0, 0), (0, 0), (0, pad), (0, 0)))
    kb = kp.reshape(B, H, n_blk, MOBA_BLOCK, d)
    vb = vp.reshape(B, H, n_blk, MOBA_BLOCK, d)
    kmean = jnp.mean(kb.astype(jnp.float32), axis=3)
    gate = jnp.einsum('bhsd,bhnd->bhsn', q.astype(jnp.float32), kmean)
    cur = jnp.arange(S) // MOBA_BLOCK
    past = jnp.arange(n_blk)[None, :] < cur[:, None]
    gate = jnp.where(past[None, None], gate, NEG)
    k_sel = min(MOBA_TOPK, n_blk)
    _, idx = lax.top_k(gate, k_sel)

    n_qc = S // Q_CHUNK
    qs = q.reshape(B, H, n_qc, Q_CHUNK, d).transpose(0, 2, 1, 3, 4).reshape(B * n_qc, H, Q_CHUNK, d)
    ids = idx.reshape(B, H, n_qc, Q_CHUNK, k_sel).transpose(0, 2, 1, 3, 4).reshape(B * n_qc, H, Q_CHUNK, k_sel)
    bids = jnp.repeat(jnp.arange(B), n_qc)
    cids = jnp.tile(jnp.arange(n_qc), B)
    bias_h = rel_bias.T.astype(jnp.float32)
    h_ix = jnp.arange(H)
    blk_ar = jnp.arange(MOBA_BLOCK)

    def one_chunk(args):
        qc, ic, b, c = args
        kb_b = kb[b]
        vb_b = vb[b]
        t = c * Q_CHUNK + jnp.arange(Q_CHUNK)
        own = (c * Q_CHUNK) // MOBA_BLOCK
        k_g = kb_b[h_ix[:, None, None], ic]
        v_g = vb_b[h_ix[:, None, None], ic]
        key_pos = ic[..., None] * MOBA_BLOCK + blk_ar
        s_sel = jnp.einsum('hqd,hqknd->hqkn', qc, k_g).astype(jnp.float32)
        s_sel = s_sel + bias_h[h_ix[:, None, None, None], t5_bucket(t[None, :, None, None] - key_pos)]
        valid = ic < own
        s_sel = jnp.where(valid[..., None], s_sel, NEG)
        k_o = lax.dynamic_index_in_dim(kb_b, own, axis=1, keepdims=False)
        v_o = lax.dynamic_index_in_dim(vb_b, own, axis=1, keepdims=False)
        dist_o = t[:, None] - (own * MOBA_BLOCK + blk_ar)[None, :]
        s_own = jnp.einsum('hqd,hnd->hqn', qc, k_o).astype(jnp.float32) + bias_h[:, t5_bucket(dist_o)]
        s_own = jnp.where((dist_o >= 0)[None], s_own, NEG)
        logits = jnp.concatenate([s_sel.reshape(H, Q_CHUNK, k_sel * MOBA_BLOCK), s_own], axis=-1)
        p = jax.nn.softmax(logits, axis=-1)
        p_sel = p[..., :k_sel * MOBA_BLOCK].reshape(H, Q_CHUNK, k_sel, MOBA_BLOCK).astype(v.dtype)
        p_own = p[..., k_sel * MOBA_BLOCK:].astype(v.dtype)
        return jnp.einsum('hqkn,hqknd->hqd', p_sel, v_g) + jnp.einsum('hqn,hnd->hqd', p_own, v_o)

    out = lax.map(one_chunk, (qs, ids, bids, cids))
    return out.reshape(B, n_qc, H, Q_CHUNK, d).transpose(0, 2, 1, 3, 4).reshape(B, H, S, d)


def mem_attention(q, mem_n, w_kv):
    B, M, _ = mem_n.shape
    kv = mem_n @ w_kv
    k, v = jnp.split(kv, 2, axis=-1)
    k = k.reshape(B, M, MEM_HEADS, MEM_HEAD_DIM)
    v = v.reshape(B, M, MEM_HEADS, MEM_HEAD_DIM)
    s = jnp.einsum('bshd,bmhd->bhsm', q, k).astype(jnp.float32) * (MEM_HEAD_DIM ** -0.5)
    p = jax.nn.softmax(s, axis=-1).astype(v.dtype)
    return jnp.einsum('bhsm,bmhd->bshd', p, v)


def setup_inputs(seed: int = 0) -> dict:
    key = jax.random.key(seed)
    ks = jax.random.split(key, 24)
    f32 = jnp.float32

    def nrm(k, shape, scale):
        return jax.random.normal(k, shape, f32) * scale

    def gain(k, shape):
        return 1.0 + 0.05 * jax.random.normal(k, shape, f32)

    L, D = DEPTH, D_MODEL
    return {
        "x": nrm(ks[0], (BATCH, SEQ, D), 1.0),
        "mem": nrm(ks[1], (BATCH, MEM_LEN, D), 1.0),
        "ln_mix_pre": gain(ks[2], (L, D)),
        "ln_mix_post": gain(ks[3], (L, D)),
        "ln_ffn_pre": gain(ks[4], (L, D)),
        "ln_ffn_post": gain(ks[5], (L, D)),
        "ln_mem": gain(ks[6], (L, D)),
        "w_in": nrm(ks[7], (L, D, IN_WIDTH), D ** -0.5),
        "ln_v_gain": gain(ks[8], (L, GMLP_WIDTH)),
        "ln_v_bias": nrm(ks[9], (L, GMLP_WIDTH), 0.05),
        "w_spatial": nrm(ks[10], (L, GMLP_GROUPS, CHUNK, CHUNK), CHUNK ** -0.5),
        "b_spatial": 1.0 + nrm(ks[11], (L, GMLP_GROUPS, CHUNK), 0.1),
        "rel_bias": nrm(ks[12], (REL_BUCKETS, MOBA_HEADS), 0.5),
        "w_mem_kv": nrm(ks[13], (L, D, 2 * MEM_WIDTH), D ** -0.5),
        "w_branch_a": nrm(ks[14], (L, GMLP_WIDTH, D), GMLP_WIDTH ** -0.5),
        "w_branch_b": nrm(ks[15], (L, MOBA_WIDTH, D), MOBA_WIDTH ** -0.5),
        "w_branch_c": nrm(ks[16], (L, MEM_WIDTH, D), MEM_WIDTH ** -0.5),
        "w_out": nrm(ks[17], (L, D, D), D ** -0.5),
        "w_ffn_gate": nrm(ks[18], (L, D, D_FF), D ** -0.5),
        "w_ffn_up": nrm(ks[19], (L, D, D_FF), D ** -0.5),
        "w_ffn_down": nrm(ks[20], (L, D_FF, D), D_FF ** -0.5),
    }


def reference(x, mem, ln_mix_pre, ln_mix_post, ln_ffn_pre, ln_ffn_post, ln_mem, w_in,
              ln_v_gain, ln_v_bias, w_spatial, b_spatial, rel_bias, w_mem_kv,
              w_branch_a, w_branch_b, w_branch_c, w_out, w_ffn_gate, w_ffn_up, w_ffn_down):
    B, S, D = x.shape
    split_at = np.cumsum([GMLP_WIDTH, GMLP_WIDTH, MOBA_WIDTH, MOBA_WIDTH, MOBA_WIDTH, MEM_WIDTH]).tolist()
    for l in range(DEPTH):
        h = rms_norm(x, ln_mix_pre[l])
        proj = h @ w_in[l]
        a_u, a_v, b_q, b_k, b_v, c_q, g_logit = jnp.split(proj, split_at, axis=-1)
        a_out = gmlp_branch(a_u, a_v, ln_v_gain[l], ln_v_bias[l], w_spatial[l], b_spatial[l])
        to_heads = lambda t: t.reshape(B, S, MOBA_HEADS, HEAD_DIM).transpose(0, 2, 1, 3)
        b_att = moba_attention(to_heads(b_q) * (HEAD_DIM ** -0.5), to_heads(b_k), to_heads(b_v), rel_bias)
        b_out = b_att.transpose(0, 2, 1, 3).reshape(B, S, MOBA_WIDTH)
        mem_n = rms_norm(mem, ln_mem[l])
        c_out = mem_attention(c_q.reshape(B, S, MEM_HEADS, MEM_HEAD_DIM), mem_n, w_mem_kv[l]).reshape(B, S, MEM_WIDTH)
        gates = jax.nn.sigmoid(g_logit).reshape(B, S, N_BRANCHES, D)
        merged = (gates[:, :, 0] * (a_out @ w_branch_a[l])
                  + gates[:, :, 1] * (b_out @ w_branch_b[l])
                  + gates[:, :, 2] * (c_out @ w_branch_c[l]))
        x = x + rms_norm(merged @ w_out[l], ln_mix_post[l])
        h2 = rms_norm(x, ln_ffn_pre[l])
        f = (jax.nn.silu(h2 @ w_ffn_gate[l]) * (h2 @ w_ffn_up[l])) @ w_ffn_down[l]
        x = x + rms_norm(f, ln_ffn_post[l])
    return x
```

```python
import math
import numpy as np
import concourse.bass as bass
import concourse.mybir as mybir
from concourse.bass_utils import run_bass_kernel_spmd
from concourse.ap import AP

F32 = mybir.dt.float32
BF16 = mybir.dt.bfloat16
AF = mybir.ActivationFunctionType
ALU = mybir.AluOpType
AX = mybir.AxisListType

D = 1024
SEQ = 2048
NSEQ = 2
T = 512
NT = SEQ // T
MEM = 256
DFF = 2816
NFF = DFF // 128
EPS = 1e-6
NEGV = -30000.0
SLOT = 4096
NSLOT = 3
C_U, C_V, C_Q, C_K, C_VV, C_CQ, C_G = 0, 768, 1536, 2304, 3072, 3840, 4352


class Buf:
    def __init__(self, name):
        self.name = name
        self.last_w = None
        self.readers = {}
        self.dma_readers = []
        self.dma_sem = None
        self.dma_cnt = 0


class Op:
    __slots__ = ("eng", "sem", "val", "is_dma")

    def __init__(self, eng, is_dma=False):
        self.eng = eng
        self.sem = None
        self.val = None
        self.is_dma = is_dma


class Prog:
    def __init__(self, nc):
        self.nc = nc
        self.E = {"pe": nc.tensor, "act": nc.scalar, "dve": nc.vector, "pool": nc.gpsimd, "sp": nc.sync}
        self.sem = {e: nc.alloc_semaphore("eng_" + e) for e in self.E}
        self.cnt = {e: 0 for e in self.E}
        self.waited = {e: {} for e in self.E}
        self.pending = {e: [] for e in self.E}
        self.nsem = 5

    def _wait(self, eng, sem, val):
        w = self.waited[eng]
        k = id(sem)
        if w.get(k, 0) >= val:
            return
        self.E[eng].wait_ge(sem, val)
        w[k] = val

    def _dep(self, eng, op, skip_same):
        if op is None:
            return
        if (not op.is_dma) and op.eng == eng and eng == "pe":
            return
        assert op.val is not None, "dependency on unsignalled op"
        self._wait(eng, op.sem, op.val)

    def _deps(self, eng, reads, writes, strict=False):
        for b in reads:
            self._dep(eng, b.last_w, False)
        for b in writes:
            self._dep(eng, b.last_w, not strict)
            for r in b.readers.values():
                self._dep(eng, r, not strict)
            for r in b.dma_readers:
                self._dep(eng, r, False)

    def op(self, eng, fn, reads=(), writes=(), sig=True):
        self._deps(eng, reads, writes)
        ins = fn(self.E[eng])
        o = Op(eng)
        o.sem = self.sem[eng]
        if sig:
            self.cnt[eng] += 1
            ins.then_inc(self.sem[eng], 1)
            o.val = self.cnt[eng]
            for p in self.pending[eng]:
                p.val = o.val
            self.pending[eng] = []
        else:
            self.pending[eng].append(o)
        for b in writes:
            b.last_w = o
            b.readers = {}
            b.dma_readers = []
        for b in reads:
            b.readers[eng] = o
        return o

    def dma(self, q, out, in_, reads=(), writes=(), owner=None, **kw):
        self._deps(q, reads, writes, strict=True)
        if owner.dma_sem is None:
            owner.dma_sem = {}
            owner.dma_cnt = {}
        if q not in owner.dma_sem:
            owner.dma_sem[q] = self.nc.alloc_semaphore("dma_%s_%s" % (owner.name, q))
            owner.dma_cnt[q] = 0
            self.nsem += 1
        owner.dma_cnt[q] += 1
        self.E[q].dma_start(out=out, in_=in_, **kw).then_inc(owner.dma_sem[q], 16)
        o = Op(q, True)
        o.sem = owner.dma_sem[q]
        o.val = 16 * owner.dma_cnt[q]
        for b in writes:
            b.last_w = o
            b.readers = {}
            b.dma_readers = []
        for b in reads:
            b.dma_readers.append(o)
        return o


def t5_bucket_np(n):
    n = np.maximum(n, 0)
    nf = np.maximum(n, 1).astype(np.float32)
    large = 16 + (np.log(nf / np.float32(16)) / np.float32(math.log(8.0)) * np.float32(16)).astype(np.int32)
    large = np.minimum(large, 31)
    return np.where(n < 16, n, large)


def host_consts():
    ident = np.eye(128, dtype=np.float32)
    tril = np.tril(np.ones((128, 128), dtype=np.float32))
    oh = np.zeros((33, 384), dtype=np.float32)
    for j in range(384):
        n = j - 127
        if n < 0:
            oh[32, j] = 1.0
        else:
            b = int(t5_bucket_np(np.array([n]))[0])
            oh[b, j] += 1.0
            oh[31, j] -= 1.0
    e = np.zeros((96, 8, 128), dtype=np.float32)
    for jj in range(3):
        for n in range(8):
            e[jj * 32 + n, n, :] = 1.0
    return ident, tril, oh, e


class _Stop(Exception):
    pass


def build(stop=None, stop_at=(0, 0)):
    nc = bass.Bass("TRN2", target_bir_lowering=False)
    P = Prog(nc)
    dump_ops = []

    def ckpt(name, items, at=None):
        if stop != name or (at is not None and tuple(at) != tuple(stop_at)):
            return
        for (label, ap, bufs) in items:
            shp = list(ap.shape)
            dt_ = ap.dtype
            d = nc.dram_tensor("dbg_" + label, shp, dt_, kind="ExternalOutput").ap()
            ob = Buf("dbg_" + label)
            dump_ops.append(P.dma("sp", d, ap, reads=bufs, owner=ob))
        raise _Stop()

    def din(name, shape):
        return nc.dram_tensor(name, list(shape), F32, kind="ExternalInput").ap()

    x_d = din("x", [NSEQ * SEQ, D])
    mem_d = din("mem", [NSEQ * MEM, D])
    g_mix_pre = din("ln_mix_pre", [1, D])
    g_mix_post = din("ln_mix_post", [1, D])
    g_ffn_pre = din("ln_ffn_pre", [1, D])
    g_ffn_post = din("ln_ffn_post", [1, D])
    g_mem = din("ln_mem", [1, D])
    w_in = din("w_in", [D, 7424])
    lnv_g = din("ln_v_gain", [1, 768])
    lnv_b = din("ln_v_bias", [1, 768])
    w_sp = din("w_spatial", [6, 128, 128])
    b_sp = din("b_spatial", [1, 768])
    relb = din("rel_bias", [32, 12])
    w_mkv = din("w_mem_kv", [D, 1024])
    w_ba = din("w_branch_a", [768, D])
    w_bb = din("w_branch_b", [768, D])
    w_bc = din("w_branch_c", [512, D])
    w_o = din("w_out", [D, D])
    w_fg = din("w_ffn_gate", [D, DFF])
    w_fu = din("w_ffn_up", [D, DFF])
    w_fd = din("w_ffn_down", [DFF, D])
    out_d = nc.dram_tensor("out", [NSEQ * SEQ, D], F32, kind="ExternalOutput").ap()

    ident_h, tril_h, oh_h, e_h = host_consts()
    ident_d = nc.inline_tensor(ident_h, "c_ident").ap()
    tril_d = nc.inline_tensor(tril_h, "c_tril").ap()
    oh_d = nc.inline_tensor(oh_h, "c_oh").ap()
    e_d = nc.inline_tensor(e_h.reshape(96, 1024), "c_e").ap()

    chunks = {}
    order = []

    def defchunk(name, pieces, grp):
        chunks[name] = dict(pieces=pieces, grp=grp, idx=len(order))
        order.append(name)

    defchunk("MK", [(w_mkv, 0, 8, 0, 512)], "M")
    defchunk("MV", [(w_mkv, 0, 8, 512, 512)], "M")
    defchunk("Q0", [(w_in, 0, 8, C_Q, 512)], "A")
    defchunk("Q1K0", [(w_in, 0, 8, C_Q + 512, 256), (w_in, 0, 8, C_K, 256)], "A")
    defchunk("K1", [(w_in, 0, 8, C_K + 256, 512)], "A")
    defchunk("V0", [(w_in, 0, 8, C_VV, 512)], "B")
    defchunk("V1U0", [(w_in, 0, 8, C_VV + 512, 256), (w_in, 0, 8, C_U, 256)], "B")
    defchunk("U1", [(w_in, 0, 8, C_U + 256, 512)], "B")
    defchunk("AV0", [(w_in, 0, 8, C_V, 512)], "C")
    defchunk("AV1CQ0", [(w_in, 0, 8, C_V + 512, 256), (w_in, 0, 8, C_CQ, 256)], "C")
    defchunk("CQ1", [(w_in, 0, 8, C_CQ + 256, 256)], "C")
    for j in range(8):
        defchunk("G%d" % j, [(w_in, 0, 8, C_G + br * 1024 + j * 128, 128) for br in range(3)], "D")
        defchunk("B%d" % j, [(w_ba, 0, 6, j * 128, 128), (w_bb, 0, 6, j * 128, 128), (w_bc, 0, 4, j * 128, 128)], "D")
    defchunk("WO0", [(w_o, 0, 8, 0, 512)], "E")
    defchunk("WO1", [(w_o, 0, 8, 512, 512)], "E")
    for jj in range(11):
        defchunk("GU%d" % jj, [(w_fg, 0, 8, jj * 256, 256), (w_fu, 0, 8, jj * 256, 256)], "F")
    KG = [(0, 8), (8, 8), (16, 6)]
    for c2 in range(2):
        for kg, (k0, kn) in enumerate(KG):
            defchunk("D%d_%d" % (c2, kg), [(w_fd, k0 * 128, kn, c2 * 512, 512)], "G")

    wscr = nc.dram_tensor("wscr", [len(order), 128, SLOT], BF16).ap()
    grpB = {}
    for name in order:
        ch = chunks[name]
        off = 0
        views = []
        for (w, r0, kc, c0, cw) in ch["pieces"]:
            views.append((off, kc, cw))
            off += kc * cw
        assert off <= SLOT
        ch["views"] = views
        ch["used"] = off
        grpB[name] = Buf("cast_" + name)
        grpB[name].dma_sem = nc.alloc_semaphore("cast_" + name)
    cast_done = set()

    def cast_group(g, after=()):
        if g in cast_done:
            return
        cast_done.add(g)
        for b_ in after:
            P._dep("pool", b_.last_w, False)
        for name in order:
            ch = chunks[name]
            if ch["grp"] != g:
                continue
            n_ = 0
            for (w, r0, kc, c0, cw), (off, _, _) in zip(ch["pieces"], ch["views"]):
                src = w[r0:r0 + kc * 128, c0:c0 + cw].rearrange("(k p) c -> p k c", p=128)
                dst = wscr[ch["idx"], :, off:off + kc * cw].rearrange("p (k c) -> p k c", c=cw)
                nc.gpsimd.dma_start(out=dst, in_=src).then_inc(grpB[name].dma_sem, 16)
                n_ += 1
            o = Op("pool", True)
            o.sem = grpB[name].dma_sem
            o.val = 16 * n_
            grpB[name].last_w = o

    def sb(name, shape, dt):
        return nc.alloc_sbuf_tensor(name, list(shape), dt)

    ring = [sb("ring%d" % i, [128, SLOT], BF16) for i in range(NSLOT)]
    ringB = [Buf("ring%d" % i) for i in range(NSLOT)]
    ring_pos = [0]

    def wload(name):
        ch = chunks[name]
        i = ring_pos[0] % NSLOT
        ring_pos[0] += 1
        P.dma("sp", ring[i][:, 0:ch["used"]], wscr[ch["idx"], :, 0:ch["used"]],
              reads=[grpB[name]], writes=[ringB[i]], owner=ringB[i])
        vs = []
        for (off, kc, cw) in ch["views"]:
            vs.append(ring[i][:, off:off + kc * cw].rearrange("p (k c) -> p k c", c=cw))
        return vs, ringB[i]

    NXS = 2
    xs = [sb("xs%d" % i, [128, D], F32) for i in range(NXS)]
    xsB = [Buf("xs%d" % i) for i in range(NXS)]
    odB = [Buf("od%d" % s) for s in range(4)]
    hT2 = [sb("hT_%d" % i, [128, 8, T], BF16) for i in range(2)]
    hTB2 = [[Buf("hT%d_%d" % (i, s)) for s in range(4)] for i in range(2)]
    hT, hTB = hT2[0], hTB2[0]
    KT = sb("KT", [128, 6, SEQ], BF16)
    KTB = [[Buf("KT%d_%d" % (tt, j)) for j in range(6)] for tt in range(NT)]
    Vaug = sb("Vaug", [128, 16, 12, 65], BF16)
    VB = [Buf("V%d" % k) for k in range(16)]
    kmT = sb("kmT", [128, 6, 16], BF16)
    kmf = sb("kmf", [128, 6, 2], F32)
    kmB = [Buf("km%d" % tt) for tt in range(NT)]
    kmfB = Buf("kmf")
    region = sb("region", [128, NFF * T], BF16)
    hidT = region[:, :].rearrange("p (k t) -> p k t", t=T)
    hidB = [Buf("hid%d" % j) for j in range(NFF)]
    QT = region[:, 0:6 * T].rearrange("p (k t) -> p k t", t=T)
    QB = hidB[0:6]
    uT = region[:, 6 * T:12 * T].rearrange("p (k t) -> p k t", t=T)
    uB = hidB[6:12]
    cqT = region[:, 12 * T:16 * T].rearrange("p (k t) -> p k t", t=T)
    cqB = hidB[12:16]
    coutT = region[:, 16 * T:20 * T].rearrange("p (k t) -> p k t", t=T)
    coB = hidB[16:20]
    aoutT = sb("aoutT", [128, 6, T], BF16)
    aoB = [Buf("ao%d" % s) for s in range(4)]
    boutT = sb("boutT", [128, 6, T], BF16)
    boB = [Buf("bo%d" % s) for s in range(4)]
    mergedT = sb("mergedT", [128, 8, T], BF16)
    mgB = [Buf("mg%d" % j) for j in range(8)]
    identb = sb("identb", [128, 128], BF16); identB = Buf("identb")
    onesb = sb("onesb", [128, 128], BF16); onesB = Buf("onesb")
    Et = sb("Et", [96, 8, 128], BF16); EB = Buf("Et")
    wsT = sb("wsT", [128, 6, 128], BF16); wsTB = Buf("wsT")
    Cg = sb("Cg", [128, 6, 128], F32); CgB = Buf("Cg")
    gamT = sb("gamT", [128, 6], F32); gamB = Buf("gamT")
    betT = sb("betT", [128, 6], F32); betB = Buf("betT")
    gpreT = sb("gpreT", [128, 8], F32); gpreB = Buf("gpreT")
    gffnT = sb("gffnT", [128, 8], F32); gffnB = Buf("gffnT")
    gmemT = sb("gmemT", [128, 8], F32); gmemB = Buf("gmemT")
    gpost_mix = sb("gpost_mix", [128, D], F32); gpmB = Buf("gpm")
    gpost_ffn = sb("gpost_ffn", [128, D], F32); gpfB = Buf("gpf")
    Dt = sb("Dt", [128, 12, 2, 128], BF16); DtB = Buf("Dt")
    epsT = sb("epsT", [128, 1], F32); epsB = Buf("eps")
    KmT = sb("KmT", [128, 4, MEM], BF16); KmB = Buf("KmT")
    Vm = sb("Vm", [128, 2, 512], BF16); VmB = Buf("Vm")
    memT = sb("memT", [128, 8, MEM], BF16); memTB = [Buf("memT0"), Buf("memT1")]
    xn = [sb("xn%d" % i, [128, D], BF16) for i in range(2)]; xnB = [Buf("xn0"), Buf("xn1")]
    junk = sb("junk", [128, 512], BF16); junkB = Buf("junk")
    st = [sb("st%d" % i, [128, 8], F32) for i in range(2)]; stB = [Buf("st0"), Buf("st1")]
    gv = [sb("gv0", [128, 768], F32)]; gvB = [Buf("gv0")]
    vn = [sb("vn%d" % i, [128, 768], BF16) for i in range(2)]; vnB = [Buf("vn0"), Buf("vn1")]
    NPT = 4
    PT = [sb("PT%d" % i, [128, 512], BF16) for i in range(NPT)]; PTB = [Buf("PT%d" % i) for i in range(NPT)]
    sig = [sb("sig%d" % i, [128, 512], F32) for i in range(3)]; sigB = [Buf("sig%d" % i) for i in range(3)]
    tmp = [sb("tmp%d" % i, [128, 512], F32) for i in range(2)]; tmpB = [Buf("tmp0"), Buf("tmp1")]
    botok = sb("botok", [128, 4, 768], BF16); botokB = [Buf("botok%d" % i) for i in range(4)]
    gs = sb("gs", [128, 96], F32); gsB = Buf("gs")
    top8 = sb("top8", [128, 96], F32); top8B = Buf("top8")
    selb = sb("selb", [128, 576], BF16); selbB = Buf("selb")
    selbT = sb("selbT", [96, 6, T], BF16); selbTB = Buf("selbT")
    rinv = [sb("rinv%d" % i, [128, 4], F32) for i in range(2)]; rinvB = [Buf("rinv0"), Buf("rinv1")]
    print("SBUF bytes remaining/partition:", nc.sbuf_bytes_remaining)

    banks = [nc.alloc_psum_tensor("bank%d" % i, [128, 512], F32) for i in range(8)]
    bankB = [Buf("bank%d" % i) for i in range(8)]
    free_q = list(range(8))

    def _consumed(i):
        b_ = bankB[i]
        return b_.last_w is None or any(k != "pe" for k in b_.readers)

    def balloc(hold=False):
        for idx, i in enumerate(free_q):
            if _consumed(i):
                free_q.pop(idx)
                if not hold:
                    free_q.append(i)
                return i
        raise RuntimeError("PSUM schedule needs more than 8 live banks")

    def brelease(i):
        free_q.append(i)

    rot = {}

    def nxt(key, n):
        rot[key] = (rot.get(key, -1) + 1) % n
        return rot[key]

    def mm(out, lhsT, rhs, start, stop, reads, writes, sig=None, nogrp=False):
        if sig is None:
            sig = stop
        if nogrp:
            return P.op("pe", lambda e: e.matmul(out, lhsT=lhsT, rhs=rhs, start=start, stop=stop, skip_group_check=True),
                        reads=reads, writes=writes, sig=sig)
        return P.op("pe", lambda e: e.matmul(out, lhsT=lhsT, rhs=rhs, start=start, stop=stop),
                    reads=reads, writes=writes, sig=sig)

    def tr(out, in_, reads, writes, sig):
        return P.op("pe", lambda e: e.transpose(out, in_, identb[:, :]), reads=list(reads) + [identB], writes=writes, sig=sig)

    def act(out, in_, func, reads, writes, **kw):
        return P.op("act", lambda e: e.activation(out=out, in_=in_, func=func, **kw), reads=reads, writes=writes)

    def dve(fn, reads, writes):
        return P.op("dve", fn, reads=reads, writes=writes)

    def pool(fn, reads, writes):
        return P.op("pool", fn, reads=reads, writes=writes)

    def rstd_from(stt, stBuf, col_in, col_sd, col_out):
        act(stt[:, col_sd:col_sd + 1], stt[:, col_in:col_in + 1], AF.Sqrt, [stBuf, epsB], [stBuf], bias=epsT[:, 0:1], scale=1.0)
        dve(lambda e: e.reciprocal(out=stt[:, col_out:col_out + 1], in_=stt[:, col_sd:col_sd + 1]), [stBuf], [stBuf])

    def norm_part(src_ap, srcB):
        k = nxt("st", 2)
        s_, sB = st[k], stB[k]
        act(junk[:, :], src_ap[:, 0:512], AF.Square, [srcB], [junkB, sB], accum_out=s_[:, 0:1])
        act(junk[:, :], src_ap[:, 512:1024], AF.Square, [srcB], [junkB, sB], accum_out=s_[:, 1:2])
        dve(lambda e: e.tensor_scalar(out=s_[:, 2:3], in0=s_[:, 0:1], scalar1=s_[:, 1:2], scalar2=1.0 / D, op0=ALU.add, op1=ALU.mult),
            [sB], [sB])
        rstd_from(s_, sB, 2, 1, 3)
        kx = nxt("xn", 2)
        dve(lambda e: e.tensor_scalar(out=xn[kx][:, :], in0=src_ap, scalar1=s_[:, 3:4], scalar2=None, op0=ALU.mult),
            [srcB, sB], [xnB[kx]])
        return kx

    def transpose_part(kx, gT, gTB, dstT_ap3, dstB):
        b = balloc()
        pb16 = banks[b][:, :].bitcast(BF16)
        for kc in range(8):
            tr(pb16[:, kc * 128:(kc + 1) * 128], xn[kx][:, kc * 128:(kc + 1) * 128], [xnB[kx]], [bankB[b]], sig=(kc == 7))
        gb = AP(gT, 0, [[8, 128], [1, 8], [0, 128]])
        dve(lambda e: e.tensor_tensor(out=dstT_ap3, in0=pb16[:, :].rearrange("p (k t) -> p k t", t=128), in1=gb, op=ALU.mult),
            [bankB[b], gTB], [dstB])

    def norm_transpose(src_ap, srcB, gT, gTB, dstT_ap3, dstB):
        transpose_part(norm_part(src_ap, srcB), gT, gTB, dstT_ap3, dstB)

    def proj_fm(wv, wB, col0, rhsT, rhsBs, nk=8):
        b = balloc()
        for kc in range(nk):
            mm(banks[b][:, :], wv[:, kc, col0:col0 + 128], rhsT[:, kc, :], kc == 0, kc == nk - 1,
               [wB] + list(rhsBs), [bankB[b]])
        return b

    try:
        P.dma("pool", identb[:, :], ident_d, writes=[identB], owner=identB)
        P.dma("pool", Et[:, :, :].rearrange("p a b -> p (a b)"), e_d, writes=[EB], owner=EB)
        P.op("dve", lambda e: e.memset(onesb[:, :], 1.0), writes=[onesB])
        P.op("dve", lambda e: e.memset(epsT[:, :], EPS), writes=[epsB])
        P.op("dve", lambda e: e.memset(selb[:, :], 0.0), writes=[selbB])
        P.op("dve", lambda e: e.memset(kmT[:, :, :], 0.0), writes=kmB)
        P.op("dve", lambda e: e.memset(Vaug[:, :, :, 64:65], 1.0), writes=VB)
        for (tile_, tB, src, n) in ((gpreT, gpreB, g_mix_pre, 8), (gffnT, gffnB, g_ffn_pre, 8), (gmemT, gmemB, g_mem, 8),
                                    (gamT, gamB, lnv_g, 6), (betT, betB, lnv_b, 6)):
            P.dma("sp", tile_[:, :], src.rearrange("o (k p) -> p (o k)", p=128), writes=[tB], owner=tB,
                  allow_slow_non_contiguous=True)
        P.dma("sp", gpost_mix[:, :], g_mix_post.partition_broadcast(128), writes=[gpmB], owner=gpmB)
        P.dma("sp", gpost_ffn[:, :], g_ffn_post.partition_broadcast(128), writes=[gpfB], owner=gpfB)

        tmpc = region[:, :].bitcast(F32)
        wsl = tmpc[:, 3072:3840].rearrange("p (g s) -> p g s", s=128); wslB = Buf("wsl")
        trl = tmpc[:, 5376:5504]; trlB = Buf("trl")
        wsm = hT[:, 0:6, 0:128]; wsmB = Buf("wsm")
        bsB_t = tmpc[:, 3840:4608]; bsBB = Buf("bsB")
        P.dma("sp", wsl[:, :, :], w_sp.rearrange("g t s -> t g s"), writes=[wslB], owner=wslB)
        P.dma("sp", trl[:, :], tril_d, writes=[trlB], owner=trlB)
        P.dma("sp", bsB_t[:, :], b_sp.partition_broadcast(128), writes=[bsBB], owner=bsBB)
        for g in range(6):
            dve(lambda e, g=g: e.tensor_tensor(out=wsm[:, g, :], in0=wsl[:, g, :], in1=trl[:, :], op=ALU.mult), [wslB, trlB], [wsmB])
        b = balloc()
        pb16 = banks[b][:, :].bitcast(BF16)
        for g in range(6):
            tr(pb16[:, g * 128:(g + 1) * 128], wsm[:, g, :], [wsmB], [bankB[b]], sig=(g == 5))
        dve(lambda e: e.tensor_copy(out=wsT[:, :, :].rearrange("p g t -> p (g t)"), in_=pb16[:, 0:768]), [bankB[b]], [wsTB])
        for (g0, g1) in ((0, 4), (4, 6)):
            b = balloc()
            for g in range(g0, g1):
                mm(banks[b][:, (g - g0) * 128:(g - g0 + 1) * 128], onesb[:, :], wsT[:, g, :], True, True, [onesB, wsTB], [bankB[b]],
                   sig=(g == g1 - 1))
            for g in range(g0, g1):
                dve(lambda e, g=g, b=b, g0=g0: e.scalar_tensor_tensor(
                    out=Cg[:, g, :], in0=banks[b][:, (g - g0) * 128:(g - g0 + 1) * 128], scalar=betT[:, g:g + 1],
                    in1=bsB_t[:, g * 128:(g + 1) * 128], op0=ALU.mult, op1=ALU.add), [bankB[b], betB, bsBB], [CgB])

        P.op("dve", lambda e: e.memset(rinv[0][:, 0:1], 0.0), reads=[wslB, trlB, bsBB, wsmB],
             writes=hidB + [hTB[0], rinvB[0]])
        cast_group("M")
        cast_group("A")
        cast_group("B")
        rba = sb("rba", [33, 12], F32); rbaB = Buf("rba")
        bo32 = boutT[:, :, :].rearrange("p a b -> p (a b)").bitcast(F32)
        ohs = bo32[0:33, 0:384]
        Fsb = bo32[0:12, 384:768]
        D32a = mergedT[:, :, :].rearrange("p a b -> p (a b)").bitcast(F32)[:, 0:1536].rearrange("p (a c) -> p a c", c=128)
        D32b = aoutT[:, :, :].rearrange("p a b -> p (a b)").bitcast(F32)[:, 0:1536].rearrange("p (a c) -> p a c", c=128)
        Mscr = nc.dram_tensor("Mscr", [12, 128 * 384], F32)
        MB = Buf("Mscr")
        P.op("dve", lambda e: e.memset(rba[:, :], NEGV), writes=[rbaB])
        P.dma("sp", rba[0:32, :], relb, writes=[rbaB], owner=rbaB)
        P.dma("sp", ohs, oh_d, writes=boB, owner=boB[0])

        def dchain_finish():
            b = balloc()
            mm(banks[b][0:12, 0:384], rba[0:33, 0:12], ohs, True, True, [rbaB] + boB, [bankB[b]])
            dve(lambda e: e.tensor_copy(out=Fsb, in_=banks[b][0:12, 0:384]), [bankB[b]], boB)
            srcF = Fsb.unsqueeze(1).broadcast_to([12, 128, 384])
            P.dma("pool", Mscr.ap().rearrange("h (k j) -> h k j", j=384), srcF, reads=boB, writes=[MB], owner=MB)
            P.dma("pool", D32a, AP(Mscr, 127, [[383, 128], [128 * 384, 12], [1, 128]]), reads=[MB], writes=mgB, owner=mgB[0])
            P.dma("pool", D32b, AP(Mscr, 127 + 128, [[383, 128], [128 * 384, 12], [1, 128]]), reads=[MB], writes=aoB, owner=aoB[0])
            dve(lambda e: e.tensor_copy(out=Dt[:, :, 0, :], in_=D32a), mgB, [DtB])
            dve(lambda e: e.tensor_copy(out=Dt[:, :, 1, :], in_=D32b), aoB, [DtB])

        ckpt('consts', [('wsT', wsT[:, :, :].rearrange('p g t -> p (g t)'), [wsTB]), ('Cg', Cg[:, :, :].rearrange('p g t -> p (g t)'), [CgB]),
                        ('gpreT', gpreT[:, :], [gpreB]),
                        ('gpm', gpost_mix[:, :], [gpmB]), ('Et', Et[:, :, :].rearrange('p a b -> p (a b)'), [EB]), ('gamT', gamT[:, :], [gamB])])
        tiles = [(bq, tt) for bq in range(NSEQ) for tt in range(NT)]

        def xs_next():
            return nxt("xs", NXS)

        def mem_stage(bq):
            for mt in range(2):
                r0 = bq * MEM + mt * 128
                k = xs_next()
                P.dma("sp", xs[k][:, :], mem_d[r0:r0 + 128, :], writes=[xsB[k]], owner=xsB[k])
                norm_transpose(xs[k][:, :], xsB[k], gmemT, gmemB, memT[:, :, mt * 128:(mt + 1) * 128], memTB[mt])
            (wk,), wkB = wload("MK")
            for hh in range(4):
                b = balloc()
                for kc in range(8):
                    mm(banks[b][:, 0:MEM], wk[:, kc, hh * 128:(hh + 1) * 128], memT[:, kc, :], kc == 0, kc == 7,
                       [wkB] + memTB, [bankB[b]])
                act(KmT[:, hh, :], banks[b][:, 0:MEM], AF.Copy, [bankB[b]], [KmB])
            (wv,), wvB = wload("MV")
            for mt in range(2):
                b = balloc()
                for kc in range(8):
                    mm(banks[b][:, :], memT[:, kc, mt * 128:(mt + 1) * 128], wv[:, kc, :], kc == 0, kc == 7,
                       [wvB, memTB[mt]], [bankB[b]])
                dve(lambda e, b=b, mt=mt: e.tensor_copy(out=Vm[:, mt, :], in_=banks[b][:, :]), [bankB[b]], [VmB])

        def stage1_A(ti, s):
            bq, tt = tiles[ti]
            tok0 = bq * SEQ + tt * T
            k = xs_next()
            P.dma("sp", xs[k][:, :], x_d[tok0 + s * 128: tok0 + (s + 1) * 128, :], writes=[xsB[k]], owner=xsB[k])
            return norm_part(xs[k][:, :], xsB[k])

        def stage1_B(ti, s, kx):
            transpose_part(kx, gpreT, gpreB, hT2[ti % 2][:, :, s * 128:(s + 1) * 128], hTB2[ti % 2][s])

        def tile_front(ti):
            bq, tt = tiles[ti]
            hT, hTB = hT2[ti % 2], hTB2[ti % 2]
            if (bq, tt) == (0, 0):
                dchain_finish()
                cast_group("C", after=[hTB[0]])
            (wq0,), wq0B = wload("Q0")
            (wq1, wk0), wq1B = wload("Q1K0")
            for j in range(6):
                wv_, wB_, c0 = (wq0, wq0B, j * 128) if j < 4 else (wq1, wq1B, (j - 4) * 128)
                b = proj_fm(wv_, wB_, c0, hT, hTB)
                dve(lambda e, b=b, j=j: e.tensor_scalar(out=QT[:, j, :], in0=banks[b][:, :], scalar1=0.125, scalar2=None,
                                                        op0=ALU.mult), [bankB[b]], [QB[j]])
            (wk1,), wk1B = wload("K1")
            for j in range(6):
                wv_, wB_, c0 = (wk0, wq1B, j * 128) if j < 2 else (wk1, wk1B, (j - 2) * 128)
                b = proj_fm(wv_, wB_, c0, hT, hTB)
                act(KT[:, j, tt * T:(tt + 1) * T], banks[b][:, :], AF.Copy, [bankB[b]], [KTB[tt][j]])
            dve(lambda e: e.reduce_sum(out=kmf[:, :, :], in_=KT[:, :, tt * T:(tt + 1) * T].rearrange("p k (b t) -> p k b t", b=2),
                                       axis=AX.X), KTB[tt], [kmfB])
            dve(lambda e: e.tensor_scalar(out=kmT[0:64, :, 2 * tt:2 * tt + 2], in0=kmf[0:64, :, :], scalar1=1.0 / 256,
                                          scalar2=None, op0=ALU.mult), [kmfB], [kmB[tt]])
            dve(lambda e: e.tensor_scalar(out=kmT[64:128, :, 8 + 2 * tt:8 + 2 * tt + 2], in0=kmf[64:128, :, :], scalar1=1.0 / 256,
                                          scalar2=None, op0=ALU.mult), [kmfB], [kmB[tt]])
            (wv0,), wv0B = wload("V0")
            (wv1, wu0), wv1B = wload("V1U0")

            def vproj_kv(s):
                kt = tt * 4 + s
                bA = balloc()
                for kc in range(8):
                    mm(banks[bA][:, :], hT[:, kc, s * 128:(s + 1) * 128], wv0[:, kc, :], kc == 0, kc == 7,
                       [wv0B, hTB[s]], [bankB[bA]])
                bB = balloc()
                for kc in range(8):
                    mm(banks[bB][:, 0:256], hT[:, kc, s * 128:(s + 1) * 128], wv1[:, kc, :], kc == 0, kc == 7,
                       [wv1B, hTB[s]], [bankB[bB]])
                act(Vaug[:, kt, 0:8, 0:64], banks[bA][:, :].rearrange("p (h d) -> p h d", d=64), AF.Copy, [bankB[bA]], [VB[kt]])
                act(Vaug[:, kt, 8:12, 0:64], banks[bB][:, 0:256].rearrange("p (h d) -> p h d", d=64), AF.Copy, [bankB[bB]],
                    [VB[kt]])

            def sel_gate(s):
                b = balloc()
                for j in range(6):
                    mm(banks[b][:, j * 16:(j + 1) * 16], QT[:, j, s * 128:(s + 1) * 128], kmT[:, j, 0:16], True, True,
                       [QB[j]] + kmB, [bankB[b]], sig=(j == 5))
                return b

            def sel_chain(s, b):
                c = 2 * tt + s // 2
                dve(lambda e, b=b: e.tensor_copy(out=gs[:, :], in_=banks[b][:, 0:96]), [bankB[b]], [gsB])
                dve(lambda e, c=c: e.memset(gs[:, :].rearrange("p (h n) -> p h n", n=8)[:, :, c:8], -1e30), [], [gsB])
                for h in range(12):
                    dve(lambda e, h=h: e.max(out=top8[:, h * 8:(h + 1) * 8], in_=gs[:, h * 8:(h + 1) * 8]), [gsB], [top8B])
                thr = AP(top8, 2, [[96, 128], [8, 12], [0, 8]])
                dve(lambda e: e.tensor_tensor(out=gs[:, :].rearrange("p (h n) -> p h n", n=8),
                                              in0=gs[:, :].rearrange("p (h n) -> p h n", n=8), in1=thr, op=ALU.is_ge),
                    [gsB, top8B], [gsB])
                sel_e_o = AP(selb, 0, [[576, 128], [96, 3], [32, 2], [1, 8]])
                sel_e_i = AP(gs, 0, [[96, 128], [32, 3], [16, 2], [1, 8]])
                dve(lambda e: e.tensor_scalar(out=sel_e_o, in0=sel_e_i, scalar1=1.0, scalar2=-NEGV, op0=ALU.subtract,
                                              op1=ALU.mult), [gsB], [selbB])
                sel_o_o = AP(selb, 64, [[576, 128], [96, 6], [1, 8]])
                sel_o_i = AP(gs, 8, [[96, 128], [16, 6], [1, 8]])
                dve(lambda e: e.tensor_scalar(out=sel_o_o, in0=sel_o_i, scalar1=1.0, scalar2=-NEGV, op0=ALU.subtract,
                                              op1=ALU.mult), [gsB], [selbB])

            def sel_trans(s):
                for half in range(2):
                    b2 = balloc()
                    p16 = banks[b2][:, :].bitcast(BF16)
                    for g in range(3):
                        gg = half * 3 + g
                        tr(p16[0:96, g * 128:(g + 1) * 128], selb[:, gg * 96:(gg + 1) * 96], [selbB], [bankB[b2]], sig=(g == 2))
                    act(selbT[:, half * 3:half * 3 + 3, s * 128:(s + 1) * 128],
                        p16[0:96, 0:384].rearrange("p (g q) -> p g q", q=128), AF.Copy, [bankB[b2]], [selbTB])

            if tt >= 2:
                vproj_kv(0)
                gb_ = sel_gate(0)
                for s in range(4):
                    if s + 1 < 4:
                        vproj_kv(s + 1)
                    sel_chain(s, gb_)
                    if s + 1 < 4:
                        gb_ = sel_gate(s + 1)
                    sel_trans(s)
            else:
                for s in range(4):
                    vproj_kv(s)
            ckpt('kvq', [('KT', KT[:, :, :].rearrange('p a b -> p (a b)'), [x_ for r_ in KTB for x_ in r_]), ('QT', QT.rearrange('p a b -> p (a b)'), QB),
                         ('Vaug', Vaug[:, :, :, :].rearrange('p a b c -> p (a b c)'), VB), ('kmT', kmT[:, :, :].rearrange('p a b -> p (a b)'), kmB)], at=(bq, tt))
            if (bq, tt) == (0, 0):
                cast_group("D", after=[QB[5]])

            nkt = 4 * tt + 4
            jobs = [(j, kt) for j in range(6) for kt in range(nkt)]
            jstate = {}
            pvbank = {}

            def qk(i):
                j, kt = jobs[i]
                a = kt - 4 * tt
                qlo = max(a, 0) * 128
                bb = [balloc(), balloc()]
                ex = [[], []]
                for hp in range(2):
                    h = 2 * j + hp
                    b = bb[hp]
                    if hp == 0:
                        mb, g3 = (j % 2) * 32, j // 2
                    else:
                        mb, g3 = 64, j
                    if tt >= 2:
                        n = kt // 2
                        if a < 0:
                            ex[hp].append((banks[b][:, 0:512], Et[mb:mb + 8, n, :], selbT[mb:mb + 8, g3, 0:512], [EB, selbTB]))
                        elif a < 2:
                            ex[hp].append((banks[b][:, 256:512], Et[mb:mb + 8, n, :], selbT[mb:mb + 8, g3, 256:512], [EB, selbTB]))
                bias = [[], []]
                for hp in range(2):
                    h = 2 * j + hp
                    b = bb[hp]
                    if a == -1:
                        bias[hp].append((banks[b][:, 0:128], identb[:, :], Dt[:, h, 1, :], [identB, DtB]))
                    if a >= 0:
                        if a < 3:
                            bias[hp].append((banks[b][:, a * 128:(a + 2) * 128], identb[:, :],
                                             Dt[:, h, :, :].rearrange("p a b -> p (a b)"), [identB, DtB]))
                        else:
                            bias[hp].append((banks[b][:, a * 128:(a + 1) * 128], identb[:, :], Dt[:, h, 0, :], [identB, DtB]))
                for hp in range(2):
                    ps = slice(hp * 64, (hp + 1) * 64)
                    nx = len(ex[hp]) + len(bias[hp])
                    mm(banks[bb[hp]][:, qlo:512], KT[ps, j, kt * 128:(kt + 1) * 128], QT[ps, j, qlo:512], True, nx == 0,
                       [KTB[kt // 4][j], QB[j]], [bankB[bb[hp]]])
                for hp in range(2):
                    for xi, (o_, l_, r_, rb_) in enumerate(ex[hp]):
                        mm(o_, l_, r_, False, len(bias[hp]) == 0 and xi == len(ex[hp]) - 1, rb_, [bankB[bb[hp]]])
                for hp in range(2):
                    for xi, (o_, l_, r_, rb_) in enumerate(bias[hp]):
                        mm(o_, l_, r_, False, xi == len(bias[hp]) - 1, rb_, [bankB[bb[hp]]])
                jstate[i] = (bb, qlo)

            qk(0)
            for i in range(len(jobs)):
                if i + 1 < len(jobs):
                    qk(i + 1)
                j, kt = jobs[i]
                bb, qlo = jstate.pop(i)
                pks = []
                for hp in range(2):
                    pk = nxt("PT", NPT)
                    pks.append(pk)
                    act(PT[pk][:, qlo:512], banks[bb[hp]][:, qlo:512], AF.Exp, [bankB[bb[hp]]], [PTB[pk]])
                if kt == 0:
                    pvbank[j] = [balloc(hold=True), balloc(hold=True)]
                s_lo = qlo // 128
                for hp in range(2):
                    h = 2 * j + hp
                    pvb = pvbank[j][hp]
                    pv = banks[pvb]
                    pk = pks[hp]
                    for s in range(s_lo, 4):
                        mm(pv[:, s * 65:(s + 1) * 65], PT[pk][:, s * 128:(s + 1) * 128], Vaug[:, kt, h, 0:65],
                           kt == 0 and s == 0, kt == 4 * tt + s, [PTB[pk], VB[kt]], [bankB[pvb]], sig=(s == 3), nogrp=True)
                if kt == nkt - 1:
                    for hp in range(2):
                        h = 2 * j + hp
                        pvb = pvbank[j][hp]
                        pv = banks[pvb]
                        rk = nxt("rinv", 2)
                        pv3 = pv[:, 0:260].rearrange("p (q d) -> p q d", d=65)
                        dve(lambda e, rk=rk, pv3=pv3: e.reciprocal(out=rinv[rk][:, :].rearrange("p (q o) -> p q o", o=1),
                                                                   in_=pv3[:, :, 64:65]), [bankB[pvb]], [rinvB[rk]])
                        rb3 = AP(rinv[rk], 0, [[4, 128], [1, 4], [0, 64]])
                        dve(lambda e, pv3=pv3, rb3=rb3, h=h: e.tensor_tensor(
                            out=botok[:, :, h * 64:(h + 1) * 64], in0=pv3[:, :, 0:64], in1=rb3, op=ALU.mult),
                            [bankB[pvb], rinvB[rk]], botokB)
                        brelease(pvb)
            for s in range(4):
                b = balloc()
                p16 = banks[b][:, :].bitcast(BF16)
                for jc in range(6):
                    tr(p16[:, jc * 128:(jc + 1) * 128], botok[:, s, jc * 128:(jc + 1) * 128], [botokB[s]], [bankB[b]], sig=(jc == 5))
                act(boutT[:, :, s * 128:(s + 1) * 128], p16[:, 0:768].rearrange("p (k t) -> p k t", t=128), AF.Copy,
                    [bankB[b]], [boB[s]])
            ckpt('attn', [('boutT', boutT[:, :, :].rearrange('p a b -> p (a b)'), boB)], at=(bq, tt))
            if (bq, tt) == (0, 0):
                cast_group("E", after=[boB[0]])
                cast_group("F", after=[boB[3]])

            (wu1,), wu1B = wload("U1")
            for j in range(6):
                wv_, wB_, c0 = (wu0, wv1B, j * 128) if j < 2 else (wu1, wu1B, (j - 2) * 128)
                b = proj_fm(wv_, wB_, c0, hT, hTB)
                act(uT[:, j, :], banks[b][:, :], AF.Gelu_apprx_tanh, [bankB[b]], [uB[j]])
            (wa0,), wa0B = wload("AV0")
            (wa1, wc0), wa1B = wload("AV1CQ0")

            def vproj(s):
                bA = balloc()
                for kc in range(8):
                    mm(banks[bA][:, :], hT[:, kc, s * 128:(s + 1) * 128], wa0[:, kc, :], kc == 0, kc == 7, [wa0B, hTB[s]], [bankB[bA]])
                bB = balloc()
                for kc in range(8):
                    mm(banks[bB][:, 0:256], hT[:, kc, s * 128:(s + 1) * 128], wa1[:, kc, :], kc == 0, kc == 7, [wa1B, hTB[s]],
                       [bankB[bB]])
                return bA, bB

            vb = {0: vproj(0), 1: vproj(1)}
            for s in range(4):
                if s + 2 < 4:
                    vb[s + 2] = vproj(s + 2)
                bA, bB = vb.pop(s)
                k = nxt("st", 2)
                s_, sB = st[k], stB[k]
                kg_ = 0
                gv_, gvB_ = gv[kg_], gvB[kg_]
                act(gv_[:, 0:512], banks[bA][:, :], AF.Gelu_apprx_tanh, [bankB[bA]], [gvB_, sB], accum_out=s_[:, 0:1])
                act(gv_[:, 512:768], banks[bB][:, 0:256], AF.Gelu_apprx_tanh, [bankB[bB]], [gvB_, sB], accum_out=s_[:, 1:2])
                act(junk[:, 0:512], gv_[:, 0:512], AF.Square, [gvB_], [junkB, sB], accum_out=s_[:, 2:3])
                act(junk[:, 0:256], gv_[:, 512:768], AF.Square, [gvB_], [junkB, sB], accum_out=s_[:, 6:7])
                dve(lambda e, s_=s_: e.tensor_scalar(out=s_[:, 3:4], in0=s_[:, 0:1], scalar1=s_[:, 1:2], scalar2=1.0 / 768,
                                                     op0=ALU.add, op1=ALU.mult), [sB], [sB])
                dve(lambda e, s_=s_: e.tensor_scalar(out=s_[:, 2:3], in0=s_[:, 2:3], scalar1=s_[:, 6:7], scalar2=1.0 / 768,
                                                     op0=ALU.add, op1=ALU.mult), [sB], [sB])
                dve(lambda e, s_=s_: e.tensor_tensor(out=s_[:, 4:5], in0=s_[:, 3:4], in1=s_[:, 3:4], op=ALU.mult), [sB], [sB])
                dve(lambda e, s_=s_: e.tensor_tensor(out=s_[:, 5:6], in0=s_[:, 2:3], in1=s_[:, 4:5], op=ALU.subtract), [sB], [sB])
                rstd_from(s_, sB, 5, 6, 7)
                dve(lambda e, s_=s_: e.scalar_tensor_tensor(out=s_[:, 4:5], in0=s_[:, 3:4], scalar=-1.0, in1=s_[:, 7:8],
                                                            op0=ALU.mult, op1=ALU.mult), [sB], [sB])
                kv = nxt("vn", 2)
                dve(lambda e, s_=s_, kv=kv, gv_=gv_: e.tensor_scalar(out=vn[kv][:, :], in0=gv_[:, :], scalar1=s_[:, 7:8],
                                                                     scalar2=s_[:, 4:5], op0=ALU.mult, op1=ALU.add),
                    [gvB_, sB], [vnB[kv]])
                for (g0, g1) in ((0, 4), (4, 6)):
                    b = balloc()
                    ng = g1 - g0
                    for g in range(g0, g1):
                        mm(banks[b][:, (g - g0) * 128:(g - g0 + 1) * 128], vn[kv][:, g * 128:(g + 1) * 128], wsT[:, g, :], True, True,
                           [vnB[kv], wsTB], [bankB[b]], sig=(g == g1 - 1))
                    kt_ = nxt("tmp", 2)
                    t3 = tmp[kt_][:, 0:ng * 128].rearrange("p (g t) -> p g t", t=128)
                    pm3 = banks[b][:, 0:ng * 128].rearrange("p (g t) -> p g t", t=128)
                    gb = AP(gamT, g0, [[6, 128], [1, ng], [0, 128]])
                    dve(lambda e, t3=t3, pm3=pm3, gb=gb: e.tensor_tensor(out=t3, in0=pm3, in1=gb, op=ALU.mult),
                        [bankB[b], gamB], [tmpB[kt_]])
                    dve(lambda e, t3=t3, g0=g0, g1=g1: e.tensor_tensor(out=t3, in0=t3, in1=Cg[:, g0:g1, :], op=ALU.add),
                        [tmpB[kt_], CgB], [tmpB[kt_]])
                    pool(lambda e, t3=t3, g0=g0, g1=g1, s=s: e.tensor_tensor(
                        out=aoutT[:, g0:g1, s * 128:(s + 1) * 128], in0=t3, in1=uT[:, g0:g1, s * 128:(s + 1) * 128], op=ALU.mult),
                        [tmpB[kt_]] + uB[g0:g1], [aoB[s]])
            ckpt('gmlp', [('aoutT', aoutT[:, :, :].rearrange('p a b -> p (a b)'), aoB), ('uT', uT.rearrange('p a b -> p (a b)'), uB)], at=(bq, tt))

            (wc1,), wc1B = wload("CQ1")
            for hh in range(4):
                wv_, wB_, c0 = (wc0, wa1B, hh * 128) if hh < 2 else (wc1, wc1B, (hh - 2) * 128)
                b = proj_fm(wv_, wB_, c0, hT, hTB)
                act(cqT[:, hh, :], banks[b][:, :], AF.Copy, [bankB[b]], [cqB[hh]])
            def mem_qk(hh):
                sc = []
                for mt in range(2):
                    b = balloc()
                    mm(banks[b][:, :], KmT[:, hh, mt * 128:(mt + 1) * 128], cqT[:, hh, :], True, True, [KmB, cqB[hh]], [bankB[b]])
                    sc.append(b)
                return sc

            scn = mem_qk(0)
            for hh in range(4):
                sc = scn
                if hh + 1 < 4:
                    scn = mem_qk(hh + 1)
                po = balloc(hold=True)
                pss = balloc(hold=True)
                for mt in range(2):
                    b = sc[mt]
                    pk = nxt("PT", NPT)
                    act(PT[pk][:, :], banks[b][:, :], AF.Exp, [bankB[b]], [PTB[pk]], scale=128.0 ** -0.5)
                    mm(banks[po][:, :], Vm[:, mt, hh * 128:(hh + 1) * 128], PT[pk][:, :], mt == 0, mt == 1, [VmB, PTB[pk]], [bankB[po]],
                       sig=True)
                    mm(banks[pss][:, :], onesb[:, :], PT[pk][:, :], mt == 0, mt == 1, [onesB, PTB[pk]], [bankB[pss]], sig=True)
                kt_ = nxt("tmp", 2)
                dve(lambda e, kt_=kt_, pss=pss: e.reciprocal(out=tmp[kt_][:, :], in_=banks[pss][:, :]), [bankB[pss]], [tmpB[kt_]])
                dve(lambda e, kt_=kt_, po=po, hh=hh: e.tensor_tensor(out=coutT[:, hh, :], in0=banks[po][:, :], in1=tmp[kt_][:, :],
                                                                      op=ALU.mult), [bankB[po], tmpB[kt_]], [coB[hh]])
                brelease(po)
                brelease(pss)
            ckpt('memattn', [('coutT', coutT.rearrange('p a b -> p (a b)'), coB)], at=(bq, tt))
            if (bq, tt) == (0, 0):
                cast_group("G", after=[coB[3]])

            for j in range(8):
                (wg0, wg1, wg2), wgB = wload("G%d" % j)
                (wba_, wbb_, wbc_), wbB = wload("B%d" % j)
                pg = []
                for wg_ in (wg0, wg1, wg2):
                    pg.append(proj_fm(wg_, wgB, 0, hT, hTB))
                pa = proj_fm(wba_, wbB, 0, aoutT, aoB, nk=6)
                pb_ = proj_fm(wbb_, wbB, 0, boutT, boB, nk=6)
                pc = proj_fm(wbc_, wbB, 0, coutT, coB, nk=4)
                for br in range(3):
                    act(sig[br][:, :], banks[pg[br]][:, :], AF.Sigmoid, [bankB[pg[br]]], [sigB[br]])
                dve(lambda e, pa=pa: e.tensor_tensor(out=tmp[0][:, :], in0=banks[pa][:, :], in1=sig[0][:, :], op=ALU.mult),
                    [bankB[pa], sigB[0]], [tmpB[0]])
                dve(lambda e, pb_=pb_: e.tensor_tensor(out=tmp[1][:, :], in0=banks[pb_][:, :], in1=sig[1][:, :], op=ALU.mult),
                    [bankB[pb_], sigB[1]], [tmpB[1]])
                dve(lambda e: e.tensor_tensor(out=tmp[0][:, :], in0=tmp[0][:, :], in1=tmp[1][:, :], op=ALU.add),
                    [tmpB[0], tmpB[1]], [tmpB[0]])
                dve(lambda e, pc=pc: e.tensor_tensor(out=tmp[1][:, :], in0=banks[pc][:, :], in1=sig[2][:, :], op=ALU.mult),
                    [bankB[pc], sigB[2]], [tmpB[1]])
                dve(lambda e, j=j: e.tensor_tensor(out=mergedT[:, j, :], in0=tmp[0][:, :], in1=tmp[1][:, :], op=ALU.add),
                    [tmpB[0], tmpB[1]], [mgB[j]])
            ckpt('merge', [('mergedT', mergedT[:, :, :].rearrange('p a b -> p (a b)'), mgB)], at=(bq, tt))

            tok0 = bq * SEQ + tt * T
            (wo0,), wo0B = wload("WO0")
            (wo1,), wo1B = wload("WO1")

            def oproj(s):
                pbs = []
                for (wo_, woB_) in ((wo0, wo0B), (wo1, wo1B)):
                    b = balloc(hold=True)
                    for kc in range(8):
                        mm(banks[b][:, :], mergedT[:, kc, s * 128:(s + 1) * 128], wo_[:, kc, :], kc == 0, kc == 7,
                           [woB_, mgB[kc]], [bankB[b]])
                    pbs.append(b)
                return pbs

            (wg0_, wu0_), wgu0B = wload("GU0")
            gu_groups = [(wg0_, 0), (wu0_, 0), (wg0_, 1), (wu0_, 1)]
            gub = []

            def gu0_slice(s):
                if not gub:
                    for _ in range(4):
                        gub.append(balloc(hold=True))
                for gi, (w_, q) in enumerate(gu_groups):
                    for kc in range(8):
                        mm(banks[gub[gi]][:, s * 128:(s + 1) * 128], w_[:, kc, q * 128:(q + 1) * 128], hT[:, kc, s * 128:(s + 1) * 128],
                           kc == 0, kc == 7, [wgu0B, hTB[s]], [bankB[gub[gi]]])

            ob = {0: oproj(0), 1: oproj(1)}
            pend = None
            for s in range(4):
                if s + 2 < 4:
                    ob[s + 2] = oproj(s + 2)
                pbs = ob.pop(s)
                kx = xs_next()
                P.dma("sp", xs[kx][:, :], x_d[tok0 + s * 128: tok0 + (s + 1) * 128, :], writes=[xsB[kx]], owner=xsB[kx])
                k = nxt("st", 2)
                s_, sB = st[k], stB[k]
                for c2 in range(2):
                    act(junk[:, :], banks[pbs[c2]][:, :], AF.Square, [bankB[pbs[c2]]], [junkB, sB], accum_out=s_[:, c2:c2 + 1])
                dve(lambda e, s_=s_: e.tensor_scalar(out=s_[:, 3:4], in0=s_[:, 0:1], scalar1=s_[:, 1:2], scalar2=1.0 / D,
                                                     op0=ALU.add, op1=ALU.mult), [sB], [sB])
                rstd_from(s_, sB, 3, 4, 5)
                for c2 in range(2):
                    kt_ = nxt("tmp", 2)
                    dve(lambda e, s_=s_, kt_=kt_, c2=c2, pbs=pbs: e.scalar_tensor_tensor(
                        out=tmp[kt_][:, :], in0=banks[pbs[c2]][:, :], scalar=s_[:, 5:6], in1=gpost_mix[:, c2 * 512:(c2 + 1) * 512],
                        op0=ALU.mult, op1=ALU.mult), [bankB[pbs[c2]], sB, gpmB], [tmpB[kt_]])
                    brelease(pbs[c2])
                    pool(lambda e, kt_=kt_, c2=c2, kx=kx: e.tensor_tensor(
                        out=xs[kx][:, c2 * 512:(c2 + 1) * 512], in0=xs[kx][:, c2 * 512:(c2 + 1) * 512], in1=tmp[kt_][:, :],
                        op=ALU.add), [tmpB[kt_], xsB[kx]], [xsB[kx]])
                P.dma("pool", out_d[tok0 + s * 128: tok0 + (s + 1) * 128, :], xs[kx][:, :], reads=[xsB[kx]], writes=[odB[s]],
                      owner=xsB[kx])
                kxn = norm_part(xs[kx][:, :], xsB[kx])
                if pend is not None:
                    transpose_part(pend[1], gffnT, gffnB, hT[:, :, pend[0] * 128:(pend[0] + 1) * 128], hTB[pend[0]])
                    gu0_slice(pend[0])
                pend = (s, kxn)
            transpose_part(pend[1], gffnT, gffnB, hT[:, :, pend[0] * 128:(pend[0] + 1) * 128], hTB[pend[0]])
            gu0_slice(pend[0])
            for q in range(2):
                pgt, pup = gub[2 * q], gub[2 * q + 1]
                ks = nxt("sig", 3)
                act(sig[ks][:, :], banks[pgt][:, :], AF.Silu, [bankB[pgt]], [sigB[ks]])
                dve(lambda e, ks=ks, pup=pup, q=q: e.tensor_tensor(out=hidT[:, q, :], in0=banks[pup][:, :], in1=sig[ks][:, :],
                                                                    op=ALU.mult), [bankB[pup], sigB[ks]], [hidB[q]])
                brelease(pgt)
                brelease(pup)
            ckpt('oproj', [('h2T', hT[:, :, :].rearrange('p a b -> p (a b)'), hTB)], at=(bq, tt))

        def tile_ffn(ti):
            bq, tt = tiles[ti]
            hT, hTB = hT2[ti % 2], hTB2[ti % 2]
            tok0 = bq * SEQ + tt * T
            nxt_new_seq = (ti + 1 < len(tiles)) and tiles[ti + 1][1] == 0
            s1k = {}
            for jj in range(1, 11):
                (wg_, wu_), wB_ = wload("GU%d" % jj)
                for q in range(2):
                    j = 2 * jj + q
                    pgt = proj_fm(wg_, wB_, q * 128, hT, hTB)
                    pup = proj_fm(wu_, wB_, q * 128, hT, hTB)
                    ks = nxt("sig", 3)
                    act(sig[ks][:, :], banks[pgt][:, :], AF.Silu, [bankB[pgt]], [sigB[ks]])
                    dve(lambda e, ks=ks, pup=pup, j=j: e.tensor_tensor(out=hidT[:, j, :], in0=banks[pup][:, :], in1=sig[ks][:, :],
                                                                        op=ALU.mult), [bankB[pup], sigB[ks]], [hidB[j]])
                if ti + 1 < len(tiles) and jj in (1, 3, 5, 7):
                    s1k[(jj - 1) // 2] = stage1_A(ti + 1, (jj - 1) // 2)
                if ti + 1 < len(tiles) and jj in (2, 4, 6, 8):
                    stage1_B(ti + 1, (jj - 2) // 2, s1k[(jj - 2) // 2])
            if nxt_new_seq:
                mem_stage(tiles[ti + 1][0])
            pd = [[None] * 4 for _ in range(2)]
            for c2 in range(2):
                for s in range(4):
                    pd[c2][s] = balloc(hold=True)
                for kg, (k0, kn) in enumerate(KG):
                    (wd_,), wdB_ = wload("D%d_%d" % (c2, kg))
                    for s in range(4):
                        for kl in range(kn):
                            kc = k0 + kl
                            mm(banks[pd[c2][s]][:, :], hidT[:, kc, s * 128:(s + 1) * 128], wd_[:, kl, :], kc == 0, kc == NFF - 1,
                               [wdB_, hidB[kc]], [bankB[pd[c2][s]]], sig=(kl == kn - 1))
            for s in range(4):
                kx = xs_next()
                P.dma("pool", xs[kx][:, :], out_d[tok0 + s * 128: tok0 + (s + 1) * 128, :], reads=[odB[s]], writes=[xsB[kx]],
                      owner=xsB[kx])
                k = nxt("st", 2)
                s_, sB = st[k], stB[k]
                for c2 in range(2):
                    act(junk[:, :], banks[pd[c2][s]][:, :], AF.Square, [bankB[pd[c2][s]]], [junkB, sB], accum_out=s_[:, c2:c2 + 1])
                dve(lambda e, s_=s_: e.tensor_scalar(out=s_[:, 3:4], in0=s_[:, 0:1], scalar1=s_[:, 1:2], scalar2=1.0 / D,
                                                     op0=ALU.add, op1=ALU.mult), [sB], [sB])
                rstd_from(s_, sB, 3, 4, 5)
                for c2 in range(2):
                    kt_ = nxt("tmp", 2)
                    dve(lambda e, s_=s_, kt_=kt_, c2=c2, s=s: e.scalar_tensor_tensor(
                        out=tmp[kt_][:, :], in0=banks[pd[c2][s]][:, :], scalar=s_[:, 5:6], in1=gpost_ffn[:, c2 * 512:(c2 + 1) * 512],
                        op0=ALU.mult, op1=ALU.mult), [bankB[pd[c2][s]], sB, gpfB], [tmpB[kt_]])
                    dve(lambda e, kt_=kt_, c2=c2, kx=kx: e.tensor_tensor(
                        out=xs[kx][:, c2 * 512:(c2 + 1) * 512], in0=xs[kx][:, c2 * 512:(c2 + 1) * 512], in1=tmp[kt_][:, :],
                        op=ALU.add), [tmpB[kt_], xsB[kx]], [xsB[kx]])
                    brelease(pd[c2][s])
                P.dma("pool", out_d[tok0 + s * 128: tok0 + (s + 1) * 128, :], xs[kx][:, :], reads=[xsB[kx]], writes=[odB[s]],
                      owner=xsB[kx])
            ckpt('ffn', [], at=(bq, tt))

        mem_stage(0)
        for s in range(4):
            stage1_B(0, s, stage1_A(0, s))
        for ti in range(len(tiles)):
            tile_front(ti)
            tile_ffn(ti)
    except _Stop:
        pass
    for o_ in dump_ops:
        P._wait('sp', o_.sem, o_.val)
    for b_ in xsB:
        for r in b_.dma_readers:
            P._wait("pool", r.sem, r.val)
    print("instr counts (signals):", P.cnt, "sems:", P.nsem)
    return nc


_NC_CACHE = {}


def kernel(**inputs):
    n = 8
    if "nc" not in _NC_CACHE:
        _NC_CACHE["nc"] = build()
    nc = _NC_CACHE["nc"]
    x = np.ascontiguousarray(inputs["x"], dtype=np.float32)
    mem = np.ascontiguousarray(inputs["mem"], dtype=np.float32)
    shared = {}
    for k in ("ln_mix_pre", "ln_mix_post", "ln_ffn_pre", "ln_ffn_post", "ln_mem", "ln_v_gain", "ln_v_bias"):
        shared[k] = np.ascontiguousarray(inputs[k], dtype=np.float32).reshape(1, -1)
    shared["w_in"] = np.ascontiguousarray(inputs["w_in"][0], dtype=np.float32)
    shared["w_spatial"] = np.ascontiguousarray(inputs["w_spatial"][0], dtype=np.float32)
    shared["b_spatial"] = np.ascontiguousarray(inputs["b_spatial"][0], dtype=np.float32).reshape(1, 768)
    shared["rel_bias"] = np.ascontiguousarray(inputs["rel_bias"], dtype=np.float32)
    for k in ("w_mem_kv", "w_branch_a", "w_branch_b", "w_branch_c", "w_out", "w_ffn_gate", "w_ffn_up", "w_ffn_down"):
        shared[k] = np.ascontiguousarray(inputs[k][0], dtype=np.float32)
    in_maps = []
    for c in range(n):
        m = dict(shared)
        m["x"] = x[2 * c:2 * c + 2].reshape(NSEQ * SEQ, D)
        m["mem"] = mem[2 * c:2 * c + 2].reshape(NSEQ * MEM, D)
        in_maps.append(m)
    res = run_bass_kernel_spmd(nc, in_maps, core_ids=list(range(n)))
    outs = [np.asarray(r["out"]).reshape(NSEQ, SEQ, D) for r in res.results]
    return np.concatenate(outs, axis=0).astype(np.float32, copy=False)
```

```python
import math
import numpy as np
import concourse.bass as bass
import concourse.mybir as mybir
from concourse.bass_utils import run_bass_kernel_spmd
from concourse.ap import AP

F32 = mybir.dt.float32
BF16 = mybir.dt.bfloat16
AF = mybir.ActivationFunctionType
ALU = mybir.AluOpType
AX = mybir.AxisListType

D = 1024
SEQ = 2048
NSEQ = 2
T = 512
NT = SEQ // T
MEM = 256
DFF = 2816
NFF = DFF // 128
EPS = 1e-6
NEGV = -30000.0
SLOT = 4096
NSLOT = 3
C_U, C_V, C_Q, C_K, C_VV, C_CQ, C_G = 0, 768, 1536, 2304, 3072, 3840, 4352


class Buf:
    def __init__(self, name):
        self.name = name
        self.last_w = None
        self.readers = {}
        self.dma_readers = []
        self.dma_sem = None
        self.dma_cnt = 0


class Op:
    __slots__ = ("eng", "sem", "val", "is_dma")

    def __init__(self, eng, is_dma=False):
        self.eng = eng
        self.sem = None
        self.val = None
        self.is_dma = is_dma


class Prog:
    def __init__(self, nc):
        self.nc = nc
        self.E = {"pe": nc.tensor, "act": nc.scalar, "dve": nc.vector, "pool": nc.gpsimd, "sp": nc.sync}
        self.sem = {e: nc.alloc_semaphore("eng_" + e) for e in self.E}
        self.cnt = {e: 0 for e in self.E}
        self.waited = {e: {} for e in self.E}
        self.pending = {e: [] for e in self.E}
        self.nsem = 5

    def _wait(self, eng, sem, val):
        w = self.waited[eng]
        k = id(sem)
        if w.get(k, 0) >= val:
            return
        self.E[eng].wait_ge(sem, val)
        w[k] = val

    def _dep(self, eng, op, skip_same):
        if op is None:
            return
        if (not op.is_dma) and op.eng == eng and eng == "pe":
            return
        assert op.val is not None, "dependency on unsignalled op"
        self._wait(eng, op.sem, op.val)

    def _deps(self, eng, reads, writes, strict=False):
        for b in reads:
            self._dep(eng, b.last_w, False)
        for b in writes:
            self._dep(eng, b.last_w, not strict)
            for r in b.readers.values():
                self._dep(eng, r, not strict)
            for r in b.dma_readers:
                self._dep(eng, r, False)

    def op(self, eng, fn, reads=(), writes=(), sig=True):
        self._deps(eng, reads, writes)
        ins = fn(self.E[eng])
        o = Op(eng)
        o.sem = self.sem[eng]
        if sig:
            self.cnt[eng] += 1
            ins.then_inc(self.sem[eng], 1)
            o.val = self.cnt[eng]
            for p in self.pending[eng]:
                p.val = o.val
            self.pending[eng] = []
        else:
            self.pending[eng].append(o)
        for b in writes:
            b.last_w = o
            b.readers = {}
            b.dma_readers = []
        for b in reads:
            b.readers[eng] = o
        return o

    def dma(self, q, out, in_, reads=(), writes=(), owner=None, **kw):
        self._deps(q, reads, writes, strict=True)
        if owner.dma_sem is None:
            owner.dma_sem = {}
            owner.dma_cnt = {}
        if q not in owner.dma_sem:
            owner.dma_sem[q] = self.nc.alloc_semaphore("dma_%s_%s" % (owner.name, q))
            owner.dma_cnt[q] = 0
            self.nsem += 1
        owner.dma_cnt[q] += 1
        self.E[q].dma_start(out=out, in_=in_, **kw).then_inc(owner.dma_sem[q], 16)
        o = Op(q, True)
        o.sem = owner.dma_sem[q]
        o.val = 16 * owner.dma_cnt[q]
        for b in writes:
            b.last_w = o
            b.readers = {}
            b.dma_readers = []
        for b in reads:
            b.dma_readers.append(o)
        return o


def t5_bucket_np(n):
    n = np.maximum(n, 0)
    nf = np.maximum(n, 1).astype(np.float32)
    large = 16 + (np.log(nf / np.float32(16)) / np.float32(math.log(8.0)) * np.float32(16)).astype(np.int32)
    large = np.minimum(large, 31)
    return np.where(n < 16, n, large)


def host_consts():
    ident = np.eye(128, dtype=np.float32)
    tril = np.tril(np.ones((128, 128), dtype=np.float32))
    oh = np.zeros((33, 384), dtype=np.float32)
    for j in range(384):
        n = j - 127
        if n < 0:
            oh[32, j] = 1.0
        else:
            b = int(t5_bucket_np(np.array([n]))[0])
            oh[b, j] += 1.0
            oh[31, j] -= 1.0
    e = np.zeros((96, 8, 128), dtype=np.float32)
    for jj in range(3):
        for n in range(8):
            e[jj * 32 + n, n, :] = 1.0
    return ident, tril, oh, e


class _Stop(Exception):
    pass


def build(stop=None, stop_at=(0, 0)):
    nc = bass.Bass("TRN2", target_bir_lowering=False)
    P = Prog(nc)
    dump_ops = []

    def ckpt(name, items, at=None):
        if stop != name or (at is not None and tuple(at) != tuple(stop_at)):
            return
        for (label, ap, bufs) in items:
            shp = list(ap.shape)
            dt_ = ap.dtype
            d = nc.dram_tensor("dbg_" + label, shp, dt_, kind="ExternalOutput").ap()
            ob = Buf("dbg_" + label)
            dump_ops.append(P.dma("sp", d, ap, reads=bufs, owner=ob))
        raise _Stop()

    def din(name, shape):
        return nc.dram_tensor(name, list(shape), F32, kind="ExternalInput").ap()

    x_d = din("x", [NSEQ * SEQ, D])
    mem_d = din("mem", [NSEQ * MEM, D])
    g_mix_pre = din("ln_mix_pre", [1, D])
    g_mix_post = din("ln_mix_post", [1, D])
    g_ffn_pre = din("ln_ffn_pre", [1, D])
    g_ffn_post = din("ln_ffn_post", [1, D])
    g_mem = din("ln_mem", [1, D])
    w_in = din("w_in", [D, 7424])
    lnv_g = din("ln_v_gain", [1, 768])
    lnv_b = din("ln_v_bias", [1, 768])
    w_sp = din("w_spatial", [6, 128, 128])
    b_sp = din("b_spatial", [1, 768])
    relb = din("rel_bias", [32, 12])
    w_mkv = din("w_mem_kv", [D, 1024])
    w_ba = din("w_branch_a", [768, D])
    w_bb = din("w_branch_b", [768, D])
    w_bc = din("w_branch_c", [512, D])
    w_o = din("w_out", [D, D])
    w_fg = din("w_ffn_gate", [D, DFF])
    w_fu = din("w_ffn_up", [D, DFF])
    w_fd = din("w_ffn_down", [DFF, D])
    out_d = nc.dram_tensor("out", [NSEQ * SEQ, D], F32, kind="ExternalOutput").ap()

    ident_h, tril_h, oh_h, e_h = host_consts()
    ident_d = nc.inline_tensor(ident_h, "c_ident").ap()
    tril_d = nc.inline_tensor(tril_h, "c_tril").ap()
    oh_d = nc.inline_tensor(oh_h, "c_oh").ap()
    e_d = nc.inline_tensor(e_h.reshape(96, 1024), "c_e").ap()

    chunks = {}
    order = []

    def defchunk(name, pieces, grp):
        chunks[name] = dict(pieces=pieces, grp=grp, idx=len(order))
        order.append(name)

    defchunk("MK", [(w_mkv, 0, 8, 0, 512)], "M")
    defchunk("MV", [(w_mkv, 0, 8, 512, 512)], "M")
    defchunk("Q0", [(w_in, 0, 8, C_Q, 512)], "A")
    defchunk("Q1K0", [(w_in, 0, 8, C_Q + 512, 256), (w_in, 0, 8, C_K, 256)], "A")
    defchunk("K1", [(w_in, 0, 8, C_K + 256, 512)], "A")
    defchunk("V0", [(w_in, 0, 8, C_VV, 512)], "B")
    defchunk("V1U0", [(w_in, 0, 8, C_VV + 512, 256), (w_in, 0, 8, C_U, 256)], "B")
    defchunk("U1", [(w_in, 0, 8, C_U + 256, 512)], "B")
    defchunk("AV0", [(w_in, 0, 8, C_V, 512)], "C")
    defchunk("AV1CQ0", [(w_in, 0, 8, C_V + 512, 256), (w_in, 0, 8, C_CQ, 256)], "C")
    defchunk("CQ1", [(w_in, 0, 8, C_CQ + 256, 256)], "C")
    for j in range(8):
        defchunk("G%d" % j, [(w_in, 0, 8, C_G + br * 1024 + j * 128, 128) for br in range(3)], "D")
        defchunk("B%d" % j, [(w_ba, 0, 6, j * 128, 128), (w_bb, 0, 6, j * 128, 128), (w_bc, 0, 4, j * 128, 128)], "D")
    defchunk("WO0", [(w_o, 0, 8, 0, 512)], "E")
    defchunk("WO1", [(w_o, 0, 8, 512, 512)], "E")
    for jj in range(11):
        defchunk("GU%d" % jj, [(w_fg, 0, 8, jj * 256, 256), (w_fu, 0, 8, jj * 256, 256)], "F")
    KG = [(0, 8), (8, 8), (16, 6)]
    for c2 in range(2):
        for kg, (k0, kn) in enumerate(KG):
            defchunk("D%d_%d" % (c2, kg), [(w_fd, k0 * 128, kn, c2 * 512, 512)], "G")

    wscr = nc.dram_tensor("wscr", [len(order), 128, SLOT], BF16).ap()
    grpB = {}
    for name in order:
        ch = chunks[name]
        off = 0
        views = []
        for (w, r0, kc, c0, cw) in ch["pieces"]:
            views.append((off, kc, cw))
            off += kc * cw
        assert off <= SLOT
        ch["views"] = views
        ch["used"] = off
        grpB[name] = Buf("cast_" + name)
        grpB[name].dma_sem = nc.alloc_semaphore("cast_" + name)
    cast_done = set()

    def cast_group(g, after=()):
        if g in cast_done:
            return
        cast_done.add(g)
        for b_ in after:
            P._dep("pool", b_.last_w, False)
        for name in order:
            ch = chunks[name]
            if ch["grp"] != g:
                continue
            n_ = 0
            for (w, r0, kc, c0, cw), (off, _, _) in zip(ch["pieces"], ch["views"]):
                src = w[r0:r0 + kc * 128, c0:c0 + cw].rearrange("(k p) c -> p k c", p=128)
                dst = wscr[ch["idx"], :, off:off + kc * cw].rearrange("p (k c) -> p k c", c=cw)
                nc.gpsimd.dma_start(out=dst, in_=src).then_inc(grpB[name].dma_sem, 16)
                n_ += 1
            o = Op("pool", True)
            o.sem = grpB[name].dma_sem
            o.val = 16 * n_
            grpB[name].last_w = o

    def sb(name, shape, dt):
        return nc.alloc_sbuf_tensor(name, list(shape), dt)

    ring = [sb("ring%d" % i, [128, SLOT], BF16) for i in range(NSLOT)]
    ringB = [Buf("ring%d" % i) for i in range(NSLOT)]
    ring_pos = [0]

    def wload(name):
        ch = chunks[name]
        i = ring_pos[0] % NSLOT
        ring_pos[0] += 1
        P.dma("sp", ring[i][:, 0:ch["used"]], wscr[ch["idx"], :, 0:ch["used"]],
              reads=[grpB[name]], writes=[ringB[i]], owner=ringB[i])
        vs = []
        for (off, kc, cw) in ch["views"]:
            vs.append(ring[i][:, off:off + kc * cw].rearrange("p (k c) -> p k c", c=cw))
        return vs, ringB[i]

    NXS = 2
    xs = [sb("xs%d" % i, [128, D], F32) for i in range(NXS)]
    xsB = [Buf("xs%d" % i) for i in range(NXS)]
    odB = [Buf("od%d" % s) for s in range(4)]
    hT2 = [sb("hT_%d" % i, [128, 8, T], BF16) for i in range(2)]
    hTB2 = [[Buf("hT%d_%d" % (i, s)) for s in range(4)] for i in range(2)]
    hT, hTB = hT2[0], hTB2[0]
    KT = sb("KT", [128, 6, SEQ], BF16)
    KTB = [[Buf("KT%d_%d" % (tt, j)) for j in range(6)] for tt in range(NT)]
    Vaug = sb("Vaug", [128, 16, 12, 65], BF16)
    VB = [Buf("V%d" % k) for k in range(16)]
    kmT = sb("kmT", [128, 6, 16], BF16)
    kmf = sb("kmf", [128, 6, 2], F32)
    kmB = [Buf("km%d" % tt) for tt in range(NT)]
    kmfB = Buf("kmf")
    region = sb("region", [128, NFF * T], BF16)
    hidT = region[:, :].rearrange("p (k t) -> p k t", t=T)
    hidB = [Buf("hid%d" % j) for j in range(NFF)]
    QT = region[:, 0:6 * T].rearrange("p (k t) -> p k t", t=T)
    QB = hidB[0:6]
    uT = region[:, 6 * T:12 * T].rearrange("p (k t) -> p k t", t=T)
    uB = hidB[6:12]
    cqT = region[:, 12 * T:16 * T].rearrange("p (k t) -> p k t", t=T)
    cqB = hidB[12:16]
    coutT = region[:, 16 * T:20 * T].rearrange("p (k t) -> p k t", t=T)
    coB = hidB[16:20]
    aoutT = sb("aoutT", [128, 6, T], BF16)
    aoB = [Buf("ao%d" % s) for s in range(4)]
    boutT = sb("boutT", [128, 6, T], BF16)
    boB = [Buf("bo%d" % s) for s in range(4)]
    mergedT = sb("mergedT", [128, 8, T], BF16)
    mgB = [Buf("mg%d" % j) for j in range(8)]
    identb = sb("identb", [128, 128], BF16); identB = Buf("identb")
    onesb = sb("onesb", [128, 128], BF16); onesB = Buf("onesb")
    Et = sb("Et", [96, 8, 128], BF16); EB = Buf("Et")
    wsT = sb("wsT", [128, 6, 128], BF16); wsTB = Buf("wsT")
    Cg = sb("Cg", [128, 6, 128], F32); CgB = Buf("Cg")
    gamT = sb("gamT", [128, 6], F32); gamB = Buf("gamT")
    betT = sb("betT", [128, 6], F32); betB = Buf("betT")
    gpreT = sb("gpreT", [128, 8], F32); gpreB = Buf("gpreT")
    gffnT = sb("gffnT", [128, 8], F32); gffnB = Buf("gffnT")
    gmemT = sb("gmemT", [128, 8], F32); gmemB = Buf("gmemT")
    gpost_mix = sb("gpost_mix", [128, D], F32); gpmB = Buf("gpm")
    gpost_ffn = sb("gpost_ffn", [128, D], F32); gpfB = Buf("gpf")
    Dt = sb("Dt", [128, 12, 2, 128], BF16); DtB = Buf("Dt")
    epsT = sb("epsT", [128, 1], F32); epsB = Buf("eps")
    KmT = sb("KmT", [128, 4, MEM], BF16); KmB = Buf("KmT")
    Vm = sb("Vm", [128, 2, 512], BF16); VmB = Buf("Vm")
    memT = sb("memT", [128, 8, MEM], BF16); memTB = [Buf("memT0"), Buf("memT1")]
    xn = [sb("xn%d" % i, [128, D], BF16) for i in range(2)]; xnB = [Buf("xn0"), Buf("xn1")]
    junk = sb("junk", [128, 512], BF16); junkB = Buf("junk")
    st = [sb("st%d" % i, [128, 8], F32) for i in range(2)]; stB = [Buf("st0"), Buf("st1")]
    gv = [sb("gv0", [128, 768], F32)]; gvB = [Buf("gv0")]
    vn = [sb("vn%d" % i, [128, 768], BF16) for i in range(2)]; vnB = [Buf("vn0"), Buf("vn1")]
    NPT = 4
    PT = [sb("PT%d" % i, [128, 512], BF16) for i in range(NPT)]; PTB = [Buf("PT%d" % i) for i in range(NPT)]
    sig = [sb("sig%d" % i, [128, 512], F32) for i in range(3)]; sigB = [Buf("sig%d" % i) for i in range(3)]
    tmp = [sb("tmp%d" % i, [128, 512], F32) for i in range(2)]; tmpB = [Buf("tmp0"), Buf("tmp1")]
    botok = sb("botok", [128, 4, 768], BF16); botokB = [Buf("botok%d" % i) for i in range(4)]
    gs = sb("gs", [128, 96], F32); gsB = Buf("gs")
    top8 = sb("top8", [128, 96], F32); top8B = Buf("top8")
    selb = sb("selb", [128, 576], BF16); selbB = Buf("selb")
    selbT = sb("selbT", [96, 6, T], BF16); selbTB = Buf("selbT")
    rinv = [sb("rinv%d" % i, [128, 4], F32) for i in range(2)]; rinvB = [Buf("rinv0"), Buf("rinv1")]
    print("SBUF bytes remaining/partition:", nc.sbuf_bytes_remaining)

    banks = [nc.alloc_psum_tensor("bank%d" % i, [128, 512], F32) for i in range(8)]
    bankB = [Buf("bank%d" % i) for i in range(8)]
    free_q = list(range(8))

    def _consumed(i):
        b_ = bankB[i]
        return b_.last_w is None or any(k != "pe" for k in b_.readers)

    def balloc(hold=False):
        for idx, i in enumerate(free_q):
            if _consumed(i):
                free_q.pop(idx)
                if not hold:
                    free_q.append(i)
                return i
        raise RuntimeError("PSUM schedule needs more than 8 live banks")

    def brelease(i):
        free_q.append(i)

    rot = {}

    def nxt(key, n):
        rot[key] = (rot.get(key, -1) + 1) % n
        return rot[key]

    def mm(out, lhsT, rhs, start, stop, reads, writes, sig=None, nogrp=False):
        if sig is None:
            sig = stop
        if nogrp:
            return P.op("pe", lambda e: e.matmul(out, lhsT=lhsT, rhs=rhs, start=start, stop=stop, skip_group_check=True),
                        reads=reads, writes=writes, sig=sig)
        return P.op("pe", lambda e: e.matmul(out, lhsT=lhsT, rhs=rhs, start=start, stop=stop),
                    reads=reads, writes=writes, sig=sig)

    def tr(out, in_, reads, writes, sig):
        return P.op("pe", lambda e: e.transpose(out, in_, identb[:, :]), reads=list(reads) + [identB], writes=writes, sig=sig)

    def act(out, in_, func, reads, writes, **kw):
        return P.op("act", lambda e: e.activation(out=out, in_=in_, func=func, **kw), reads=reads, writes=writes)

    def dve(fn, reads, writes):
        return P.op("dve", fn, reads=reads, writes=writes)

    def pool(fn, reads, writes):
        return P.op("pool", fn, reads=reads, writes=writes)

    def rstd_from(stt, stBuf, col_in, col_sd, col_out):
        act(stt[:, col_sd:col_sd + 1], stt[:, col_in:col_in + 1], AF.Sqrt, [stBuf, epsB], [stBuf], bias=epsT[:, 0:1], scale=1.0)
        dve(lambda e: e.reciprocal(out=stt[:, col_out:col_out + 1], in_=stt[:, col_sd:col_sd + 1]), [stBuf], [stBuf])

    def norm_part(src_ap, srcB):
        k = nxt("st", 2)
        s_, sB = st[k], stB[k]
        act(junk[:, :], src_ap[:, 0:512], AF.Square, [srcB], [junkB, sB], accum_out=s_[:, 0:1])
        act(junk[:, :], src_ap[:, 512:1024], AF.Square, [srcB], [junkB, sB], accum_out=s_[:, 1:2])
        dve(lambda e: e.tensor_scalar(out=s_[:, 2:3], in0=s_[:, 0:1], scalar1=s_[:, 1:2], scalar2=1.0 / D, op0=ALU.add, op1=ALU.mult),
            [sB], [sB])
        rstd_from(s_, sB, 2, 1, 3)
        kx = nxt("xn", 2)
        dve(lambda e: e.tensor_scalar(out=xn[kx][:, :], in0=src_ap, scalar1=s_[:, 3:4], scalar2=None, op0=ALU.mult),
            [srcB, sB], [xnB[kx]])
        return kx

    def transpose_part(kx, gT, gTB, dstT_ap3, dstB):
        b = balloc()
        pb16 = banks[b][:, :].bitcast(BF16)
        for kc in range(8):
            tr(pb16[:, kc * 128:(kc + 1) * 128], xn[kx][:, kc * 128:(kc + 1) * 128], [xnB[kx]], [bankB[b]], sig=(kc == 7))
        gb = AP(gT, 0, [[8, 128], [1, 8], [0, 128]])
        dve(lambda e: e.tensor_tensor(out=dstT_ap3, in0=pb16[:, :].rearrange("p (k t) -> p k t", t=128), in1=gb, op=ALU.mult),
            [bankB[b], gTB], [dstB])

    def norm_transpose(src_ap, srcB, gT, gTB, dstT_ap3, dstB):
        transpose_part(norm_part(src_ap, srcB), gT, gTB, dstT_ap3, dstB)

    def proj_fm(wv, wB, col0, rhsT, rhsBs, nk=8):
        b = balloc()
        for kc in range(nk):
            mm(banks[b][:, :], wv[:, kc, col0:col0 + 128], rhsT[:, kc, :], kc == 0, kc == nk - 1,
               [wB] + list(rhsBs), [bankB[b]])
        return b

    try:
        P.dma("pool", identb[:, :], ident_d, writes=[identB], owner=identB)
        P.dma("pool", Et[:, :, :].rearrange("p a b -> p (a b)"), e_d, writes=[EB], owner=EB)
        P.op("dve", lambda e: e.memset(onesb[:, :], 1.0), writes=[onesB])
        P.op("dve", lambda e: e.memset(epsT[:, :], EPS), writes=[epsB])
        P.op("dve", lambda e: e.memset(selb[:, :], 0.0), writes=[selbB])
        P.op("dve", lambda e: e.memset(kmT[:, :, :], 0.0), writes=kmB)
        P.op("dve", lambda e: e.memset(Vaug[:, :, :, 64:65], 1.0), writes=VB)
        for (tile_, tB, src, n) in ((gpreT, gpreB, g_mix_pre, 8), (gffnT, gffnB, g_ffn_pre, 8), (gmemT, gmemB, g_mem, 8),
                                    (gamT, gamB, lnv_g, 6), (betT, betB, lnv_b, 6)):
            P.dma("sp", tile_[:, :], src.rearrange("o (k p) -> p (o k)", p=128), writes=[tB], owner=tB,
                  allow_slow_non_contiguous=True)
        P.dma("sp", gpost_mix[:, :], g_mix_post.partition_broadcast(128), writes=[gpmB], owner=gpmB)
        P.dma("sp", gpost_ffn[:, :], g_ffn_post.partition_broadcast(128), writes=[gpfB], owner=gpfB)

        tmpc = region[:, :].bitcast(F32)
        wsl = tmpc[:, 3072:3840].rearrange("p (g s) -> p g s", s=128); wslB = Buf("wsl")
        trl = tmpc[:, 5376:5504]; trlB = Buf("trl")
        wsm = hT[:, 0:6, 0:128]; wsmB = Buf("wsm")
        bsB_t = tmpc[:, 3840:4608]; bsBB = Buf("bsB")
        P.dma("sp", wsl[:, :, :], w_sp.rearrange("g t s -> t g s"), writes=[wslB], owner=wslB)
        P.dma("sp", trl[:, :], tril_d, writes=[trlB], owner=trlB)
        P.dma("sp", bsB_t[:, :], b_sp.partition_broadcast(128), writes=[bsBB], owner=bsBB)
        for g in range(6):
            dve(lambda e, g=g: e.tensor_tensor(out=wsm[:, g, :], in0=wsl[:, g, :], in1=trl[:, :], op=ALU.mult), [wslB, trlB], [wsmB])
        b = balloc()
        pb16 = banks[b][:, :].bitcast(BF16)
        for g in range(6):
            tr(pb16[:, g * 128:(g + 1) * 128], wsm[:, g, :], [wsmB], [bankB[b]], sig=(g == 5))
        dve(lambda e: e.tensor_copy(out=wsT[:, :, :].rearrange("p g t -> p (g t)"), in_=pb16[:, 0:768]), [bankB[b]], [wsTB])
        for (g0, g1) in ((0, 4), (4, 6)):
            b = balloc()
            for g in range(g0, g1):
                mm(banks[b][:, (g - g0) * 128:(g - g0 + 1) * 128], onesb[:, :], wsT[:, g, :], True, True, [onesB, wsTB], [bankB[b]],
                   sig=(g == g1 - 1))
            for g in range(g0, g1):
                dve(lambda e, g=g, b=b, g0=g0: e.scalar_tensor_tensor(
                    out=Cg[:, g, :], in0=banks[b][:, (g - g0) * 128:(g - g0 + 1) * 128], scalar=betT[:, g:g + 1],
                    in1=bsB_t[:, g * 128:(g + 1) * 128], op0=ALU.mult, op1=ALU.add), [bankB[b], betB, bsBB], [CgB])

        P.op("dve", lambda e: e.memset(rinv[0][:, 0:1], 0.0), reads=[wslB, trlB, bsBB, wsmB],
             writes=hidB + [hTB[0], rinvB[0]])
        for g_ in "MABCDEFG":
            cast_group(g_)
        rba = sb("rba", [33, 12], F32); rbaB = Buf("rba")
        bo32 = boutT[:, :, :].rearrange("p a b -> p (a b)").bitcast(F32)
        ohs = bo32[0:33, 0:384]
        Fsb = bo32[0:12, 384:768]
        D32a = mergedT[:, :, :].rearrange("p a b -> p (a b)").bitcast(F32)[:, 0:1536].rearrange("p (a c) -> p a c", c=128)
        D32b = aoutT[:, :, :].rearrange("p a b -> p (a b)").bitcast(F32)[:, 0:1536].rearrange("p (a c) -> p a c", c=128)
        Mscr = nc.dram_tensor("Mscr", [12, 128 * 384], F32)
        MB = Buf("Mscr")
        P.op("dve", lambda e: e.memset(rba[:, :], NEGV), writes=[rbaB])
        P.dma("sp", rba[0:32, :], relb, writes=[rbaB], owner=rbaB)
        P.dma("sp", ohs, oh_d, writes=boB, owner=boB[0])

        def dchain_finish():
            b = balloc()
            mm(banks[b][0:12, 0:384], rba[0:33, 0:12], ohs, True, True, [rbaB] + boB, [bankB[b]])
            dve(lambda e: e.tensor_copy(out=Fsb, in_=banks[b][0:12, 0:384]), [bankB[b]], boB)
            srcF = Fsb.unsqueeze(1).broadcast_to([12, 128, 384])
            P.dma("pool", Mscr.ap().rearrange("h (k j) -> h k j", j=384), srcF, reads=boB, writes=[MB], owner=MB)
            P.dma("pool", D32a, AP(Mscr, 127, [[383, 128], [128 * 384, 12], [1, 128]]), reads=[MB], writes=mgB, owner=mgB[0])
            P.dma("pool", D32b, AP(Mscr, 127 + 128, [[383, 128], [128 * 384, 12], [1, 128]]), reads=[MB], writes=aoB, owner=aoB[0])
            dve(lambda e: e.tensor_copy(out=Dt[:, :, 0, :], in_=D32a), mgB, [DtB])
            dve(lambda e: e.tensor_copy(out=Dt[:, :, 1, :], in_=D32b), aoB, [DtB])

        ckpt('consts', [('wsT', wsT[:, :, :].rearrange('p g t -> p (g t)'), [wsTB]), ('Cg', Cg[:, :, :].rearrange('p g t -> p (g t)'), [CgB]),
                        ('gpreT', gpreT[:, :], [gpreB]),
                        ('gpm', gpost_mix[:, :], [gpmB]), ('Et', Et[:, :, :].rearrange('p a b -> p (a b)'), [EB]), ('gamT', gamT[:, :], [gamB])])
        tiles = [(bq, tt) for bq in range(NSEQ) for tt in range(NT)]

        def xs_next():
            return nxt("xs", NXS)

        def mem_stage(bq):
            for mt in range(2):
                r0 = bq * MEM + mt * 128
                k = xs_next()
                P.dma("sp", xs[k][:, :], mem_d[r0:r0 + 128, :], writes=[xsB[k]], owner=xsB[k])
                norm_transpose(xs[k][:, :], xsB[k], gmemT, gmemB, memT[:, :, mt * 128:(mt + 1) * 128], memTB[mt])
            (wk,), wkB = wload("MK")
            for hh in range(4):
                b = balloc()
                for kc in range(8):
                    mm(banks[b][:, 0:MEM], wk[:, kc, hh * 128:(hh + 1) * 128], memT[:, kc, :], kc == 0, kc == 7,
                       [wkB] + memTB, [bankB[b]])
                act(KmT[:, hh, :], banks[b][:, 0:MEM], AF.Copy, [bankB[b]], [KmB])
            (wv,), wvB = wload("MV")
            for mt in range(2):
                b = balloc()
                for kc in range(8):
                    mm(banks[b][:, :], memT[:, kc, mt * 128:(mt + 1) * 128], wv[:, kc, :], kc == 0, kc == 7,
                       [wvB, memTB[mt]], [bankB[b]])
                dve(lambda e, b=b, mt=mt: e.tensor_copy(out=Vm[:, mt, :], in_=banks[b][:, :]), [bankB[b]], [VmB])

        def stage1_A(ti, s):
            bq, tt = tiles[ti]
            tok0 = bq * SEQ + tt * T
            k = xs_next()
            P.dma("sp", xs[k][:, :], x_d[tok0 + s * 128: tok0 + (s + 1) * 128, :], writes=[xsB[k]], owner=xsB[k])
            return norm_part(xs[k][:, :], xsB[k])

        def stage1_B(ti, s, kx):
            transpose_part(kx, gpreT, gpreB, hT2[ti % 2][:, :, s * 128:(s + 1) * 128], hTB2[ti % 2][s])

        def tile_front(ti):
            bq, tt = tiles[ti]
            hT, hTB = hT2[ti % 2], hTB2[ti % 2]
            if (bq, tt) == (0, 0):
                dchain_finish()
                cast_group("C", after=[hTB[0]])
            (wq0,), wq0B = wload("Q0")
            (wq1, wk0), wq1B = wload("Q1K0")
            for j in range(6):
                wv_, wB_, c0 = (wq0, wq0B, j * 128) if j < 4 else (wq1, wq1B, (j - 4) * 128)
                b = proj_fm(wv_, wB_, c0, hT, hTB)
                dve(lambda e, b=b, j=j: e.tensor_scalar(out=QT[:, j, :], in0=banks[b][:, :], scalar1=0.125, scalar2=None,
                                                        op0=ALU.mult), [bankB[b]], [QB[j]])
            (wk1,), wk1B = wload("K1")
            for j in range(6):
                wv_, wB_, c0 = (wk0, wq1B, j * 128) if j < 2 else (wk1, wk1B, (j - 2) * 128)
                b = proj_fm(wv_, wB_, c0, hT, hTB)
                act(KT[:, j, tt * T:(tt + 1) * T], banks[b][:, :], AF.Copy, [bankB[b]], [KTB[tt][j]])
            dve(lambda e: e.reduce_sum(out=kmf[:, :, :], in_=KT[:, :, tt * T:(tt + 1) * T].rearrange("p k (b t) -> p k b t", b=2),
                                       axis=AX.X), KTB[tt], [kmfB])
            dve(lambda e: e.tensor_scalar(out=kmT[0:64, :, 2 * tt:2 * tt + 2], in0=kmf[0:64, :, :], scalar1=1.0 / 256,
                                          scalar2=None, op0=ALU.mult), [kmfB], [kmB[tt]])
            dve(lambda e: e.tensor_scalar(out=kmT[64:128, :, 8 + 2 * tt:8 + 2 * tt + 2], in0=kmf[64:128, :, :], scalar1=1.0 / 256,
                                          scalar2=None, op0=ALU.mult), [kmfB], [kmB[tt]])
            (wv0,), wv0B = wload("V0")
            (wv1, wu0), wv1B = wload("V1U0")

            def vproj_kv(s):
                kt = tt * 4 + s
                bA = balloc()
                for kc in range(8):
                    mm(banks[bA][:, :], hT[:, kc, s * 128:(s + 1) * 128], wv0[:, kc, :], kc == 0, kc == 7,
                       [wv0B, hTB[s]], [bankB[bA]])
                bB = balloc()
                for kc in range(8):
                    mm(banks[bB][:, 0:256], hT[:, kc, s * 128:(s + 1) * 128], wv1[:, kc, :], kc == 0, kc == 7,
                       [wv1B, hTB[s]], [bankB[bB]])
                act(Vaug[:, kt, 0:8, 0:64], banks[bA][:, :].rearrange("p (h d) -> p h d", d=64), AF.Copy, [bankB[bA]], [VB[kt]])
                act(Vaug[:, kt, 8:12, 0:64], banks[bB][:, 0:256].rearrange("p (h d) -> p h d", d=64), AF.Copy, [bankB[bB]],
                    [VB[kt]])

            def sel_gate(s):
                b = balloc()
                for j in range(6):
                    mm(banks[b][:, j * 16:(j + 1) * 16], QT[:, j, s * 128:(s + 1) * 128], kmT[:, j, 0:16], True, True,
                       [QB[j]] + kmB, [bankB[b]], sig=(j == 5))
                return b

            def sel_chain(s, b):
                c = 2 * tt + s // 2
                dve(lambda e, b=b: e.tensor_copy(out=gs[:, :], in_=banks[b][:, 0:96]), [bankB[b]], [gsB])
                dve(lambda e, c=c: e.memset(gs[:, :].rearrange("p (h n) -> p h n", n=8)[:, :, c:8], -1e30), [], [gsB])
                for h in range(12):
                    dve(lambda e, h=h: e.max(out=top8[:, h * 8:(h + 1) * 8], in_=gs[:, h * 8:(h + 1) * 8]), [gsB], [top8B])
                thr = AP(top8, 2, [[96, 128], [8, 12], [0, 8]])
                dve(lambda e: e.tensor_tensor(out=gs[:, :].rearrange("p (h n) -> p h n", n=8),
                                              in0=gs[:, :].rearrange("p (h n) -> p h n", n=8), in1=thr, op=ALU.is_ge),
                    [gsB, top8B], [gsB])
                sel_e_o = AP(selb, 0, [[576, 128], [96, 3], [32, 2], [1, 8]])
                sel_e_i = AP(gs, 0, [[96, 128], [32, 3], [16, 2], [1, 8]])
                dve(lambda e: e.tensor_scalar(out=sel_e_o, in0=sel_e_i, scalar1=1.0, scalar2=-NEGV, op0=ALU.subtract,
                                              op1=ALU.mult), [gsB], [selbB])
                sel_o_o = AP(selb, 64, [[576, 128], [96, 6], [1, 8]])
                sel_o_i = AP(gs, 8, [[96, 128], [16, 6], [1, 8]])
                dve(lambda e: e.tensor_scalar(out=sel_o_o, in0=sel_o_i, scalar1=1.0, scalar2=-NEGV, op0=ALU.subtract,
                                              op1=ALU.mult), [gsB], [selbB])

            def sel_trans(s):
                for half in range(2):
                    b2 = balloc()
                    p16 = banks[b2][:, :].bitcast(BF16)
                    for g in range(3):
                        gg = half * 3 + g
                        tr(p16[0:96, g * 128:(g + 1) * 128], selb[:, gg * 96:(gg + 1) * 96], [selbB], [bankB[b2]], sig=(g == 2))
                    act(selbT[:, half * 3:half * 3 + 3, s * 128:(s + 1) * 128],
                        p16[0:96, 0:384].rearrange("p (g q) -> p g q", q=128), AF.Copy, [bankB[b2]], [selbTB])

            if tt >= 2:
                vproj_kv(0)
                gb_ = sel_gate(0)
                for s in range(4):
                    if s + 1 < 4:
                        vproj_kv(s + 1)
                    sel_chain(s, gb_)
                    if s + 1 < 4:
                        gb_ = sel_gate(s + 1)
                    sel_trans(s)
            else:
                for s in range(4):
                    vproj_kv(s)
            ckpt('kvq', [('KT', KT[:, :, :].rearrange('p a b -> p (a b)'), [x_ for r_ in KTB for x_ in r_]), ('QT', QT.rearrange('p a b -> p (a b)'), QB),
                         ('Vaug', Vaug[:, :, :, :].rearrange('p a b c -> p (a b c)'), VB), ('kmT', kmT[:, :, :].rearrange('p a b -> p (a b)'), kmB)], at=(bq, tt))
            if (bq, tt) == (0, 0):
                cast_group("D", after=[QB[5]])

            nkt = 4 * tt + 4
            jobs = [(j, kt) for j in range(6) for kt in range(nkt)]
            jstate = {}
            pvbank = {}

            def qk(i):
                j, kt = jobs[i]
                a = kt - 4 * tt
                qlo = max(a, 0) * 128
                bb = [balloc(), balloc()]
                ex = [[], []]
                for hp in range(2):
                    h = 2 * j + hp
                    b = bb[hp]
                    if hp == 0:
                        mb, g3 = (j % 2) * 32, j // 2
                    else:
                        mb, g3 = 64, j
                    if tt >= 2:
                        n = kt // 2
                        if a < 0:
                            ex[hp].append((banks[b][:, 0:512], Et[mb:mb + 8, n, :], selbT[mb:mb + 8, g3, 0:512], [EB, selbTB]))
                        elif a < 2:
                            ex[hp].append((banks[b][:, 256:512], Et[mb:mb + 8, n, :], selbT[mb:mb + 8, g3, 256:512], [EB, selbTB]))
                bias = [[], []]
                for hp in range(2):
                    h = 2 * j + hp
                    b = bb[hp]
                    if a == -1:
                        bias[hp].append((banks[b][:, 0:128], identb[:, :], Dt[:, h, 1, :], [identB, DtB]))
                    if a >= 0:
                        if a < 3:
                            bias[hp].append((banks[b][:, a * 128:(a + 2) * 128], identb[:, :],
                                             Dt[:, h, :, :].rearrange("p a b -> p (a b)"), [identB, DtB]))
                        else:
                            bias[hp].append((banks[b][:, a * 128:(a + 1) * 128], identb[:, :], Dt[:, h, 0, :], [identB, DtB]))
                for hp in range(2):
                    ps = slice(hp * 64, (hp + 1) * 64)
                    nx = len(ex[hp]) + len(bias[hp])
                    mm(banks[bb[hp]][:, qlo:512], KT[ps, j, kt * 128:(kt + 1) * 128], QT[ps, j, qlo:512], True, nx == 0,
                       [KTB[kt // 4][j], QB[j]], [bankB[bb[hp]]])
                for hp in range(2):
                    for xi, (o_, l_, r_, rb_) in enumerate(ex[hp]):
                        mm(o_, l_, r_, False, len(bias[hp]) == 0 and xi == len(ex[hp]) - 1, rb_, [bankB[bb[hp]]])
                for hp in range(2):
                    for xi, (o_, l_, r_, rb_) in enumerate(bias[hp]):
                        mm(o_, l_, r_, False, xi == len(bias[hp]) - 1, rb_, [bankB[bb[hp]]])
                jstate[i] = (bb, qlo)

            qk(0)
            for i in range(len(jobs)):
                if i + 1 < len(jobs):
                    qk(i + 1)
                j, kt = jobs[i]
                bb, qlo = jstate.pop(i)
                pks = []
                for hp in range(2):
                    pk = nxt("PT", NPT)
                    pks.append(pk)
                    act(PT[pk][:, qlo:512], banks[bb[hp]][:, qlo:512], AF.Exp, [bankB[bb[hp]]], [PTB[pk]])
                if kt == 0:
                    pvbank[j] = [balloc(hold=True), balloc(hold=True)]
                s_lo = qlo // 128
                for hp in range(2):
                    h = 2 * j + hp
                    pvb = pvbank[j][hp]
                    pv = banks[pvb]
                    pk = pks[hp]
                    for s in range(s_lo, 4):
                        mm(pv[:, s * 65:(s + 1) * 65], PT[pk][:, s * 128:(s + 1) * 128], Vaug[:, kt, h, 0:65],
                           kt == 0 and s == 0, kt == 4 * tt + s, [PTB[pk], VB[kt]], [bankB[pvb]], sig=(s == 3), nogrp=True)
                if kt == nkt - 1:
                    for hp in range(2):
                        h = 2 * j + hp
                        pvb = pvbank[j][hp]
                        pv = banks[pvb]
                        rk = nxt("rinv", 2)
                        pv3 = pv[:, 0:260].rearrange("p (q d) -> p q d", d=65)
                        dve(lambda e, rk=rk, pv3=pv3: e.reciprocal(out=rinv[rk][:, :].rearrange("p (q o) -> p q o", o=1),
                                                                   in_=pv3[:, :, 64:65]), [bankB[pvb]], [rinvB[rk]])
                        rb3 = AP(rinv[rk], 0, [[4, 128], [1, 4], [0, 64]])
                        dve(lambda e, pv3=pv3, rb3=rb3, h=h: e.tensor_tensor(
                            out=botok[:, :, h * 64:(h + 1) * 64], in0=pv3[:, :, 0:64], in1=rb3, op=ALU.mult),
                            [bankB[pvb], rinvB[rk]], botokB)
                        brelease(pvb)
            for s in range(4):
                b = balloc()
                p16 = banks[b][:, :].bitcast(BF16)
                for jc in range(6):
                    tr(p16[:, jc * 128:(jc + 1) * 128], botok[:, s, jc * 128:(jc + 1) * 128], [botokB[s]], [bankB[b]], sig=(jc == 5))
                act(boutT[:, :, s * 128:(s + 1) * 128], p16[:, 0:768].rearrange("p (k t) -> p k t", t=128), AF.Copy,
                    [bankB[b]], [boB[s]])
            ckpt('attn', [('boutT', boutT[:, :, :].rearrange('p a b -> p (a b)'), boB)], at=(bq, tt))
            if (bq, tt) == (0, 0):
                cast_group("E", after=[boB[0]])
                cast_group("F", after=[boB[3]])

            (wu1,), wu1B = wload("U1")
            for j in range(6):
                wv_, wB_, c0 = (wu0, wv1B, j * 128) if j < 2 else (wu1, wu1B, (j - 2) * 128)
                b = proj_fm(wv_, wB_, c0, hT, hTB)
                act(uT[:, j, :], banks[b][:, :], AF.Gelu_apprx_tanh, [bankB[b]], [uB[j]])
            (wa0,), wa0B = wload("AV0")
            (wa1, wc0), wa1B = wload("AV1CQ0")

            def vproj(s):
                bA = balloc()
                for kc in range(8):
                    mm(banks[bA][:, :], hT[:, kc, s * 128:(s + 1) * 128], wa0[:, kc, :], kc == 0, kc == 7, [wa0B, hTB[s]], [bankB[bA]])
                bB = balloc()
                for kc in range(8):
                    mm(banks[bB][:, 0:256], hT[:, kc, s * 128:(s + 1) * 128], wa1[:, kc, :], kc == 0, kc == 7, [wa1B, hTB[s]],
                       [bankB[bB]])
                return bA, bB

            vb = {0: vproj(0), 1: vproj(1)}
            for s in range(4):
                if s + 2 < 4:
                    vb[s + 2] = vproj(s + 2)
                bA, bB = vb.pop(s)
                k = nxt("st", 2)
                s_, sB = st[k], stB[k]
                kg_ = 0
                gv_, gvB_ = gv[kg_], gvB[kg_]
                act(gv_[:, 0:512], banks[bA][:, :], AF.Gelu_apprx_tanh, [bankB[bA]], [gvB_, sB], accum_out=s_[:, 0:1])
                act(gv_[:, 512:768], banks[bB][:, 0:256], AF.Gelu_apprx_tanh, [bankB[bB]], [gvB_, sB], accum_out=s_[:, 1:2])
                act(junk[:, 0:512], gv_[:, 0:512], AF.Square, [gvB_], [junkB, sB], accum_out=s_[:, 2:3])
                act(junk[:, 0:256], gv_[:, 512:768], AF.Square, [gvB_], [junkB, sB], accum_out=s_[:, 6:7])
                dve(lambda e, s_=s_: e.tensor_scalar(out=s_[:, 3:4], in0=s_[:, 0:1], scalar1=s_[:, 1:2], scalar2=1.0 / 768,
                                                     op0=ALU.add, op1=ALU.mult), [sB], [sB])
                dve(lambda e, s_=s_: e.tensor_scalar(out=s_[:, 2:3], in0=s_[:, 2:3], scalar1=s_[:, 6:7], scalar2=1.0 / 768,
                                                     op0=ALU.add, op1=ALU.mult), [sB], [sB])
                dve(lambda e, s_=s_: e.tensor_tensor(out=s_[:, 4:5], in0=s_[:, 3:4], in1=s_[:, 3:4], op=ALU.mult), [sB], [sB])
                dve(lambda e, s_=s_: e.tensor_tensor(out=s_[:, 5:6], in0=s_[:, 2:3], in1=s_[:, 4:5], op=ALU.subtract), [sB], [sB])
                rstd_from(s_, sB, 5, 6, 7)
                dve(lambda e, s_=s_: e.scalar_tensor_tensor(out=s_[:, 4:5], in0=s_[:, 3:4], scalar=-1.0, in1=s_[:, 7:8],
                                                            op0=ALU.mult, op1=ALU.mult), [sB], [sB])
                kv = nxt("vn", 2)
                dve(lambda e, s_=s_, kv=kv, gv_=gv_: e.tensor_scalar(out=vn[kv][:, :], in0=gv_[:, :], scalar1=s_[:, 7:8],
                                                                     scalar2=s_[:, 4:5], op0=ALU.mult, op1=ALU.add),
                    [gvB_, sB], [vnB[kv]])
                for (g0, g1) in ((0, 4), (4, 6)):
                    b = balloc()
                    ng = g1 - g0
                    for g in range(g0, g1):
                        mm(banks[b][:, (g - g0) * 128:(g - g0 + 1) * 128], vn[kv][:, g * 128:(g + 1) * 128], wsT[:, g, :], True, True,
                           [vnB[kv], wsTB], [bankB[b]], sig=(g == g1 - 1))
                    kt_ = nxt("tmp", 2)
                    t3 = tmp[kt_][:, 0:ng * 128].rearrange("p (g t) -> p g t", t=128)
                    pm3 = banks[b][:, 0:ng * 128].rearrange("p (g t) -> p g t", t=128)
                    gb = AP(gamT, g0, [[6, 128], [1, ng], [0, 128]])
                    dve(lambda e, t3=t3, pm3=pm3, gb=gb: e.tensor_tensor(out=t3, in0=pm3, in1=gb, op=ALU.mult),
                        [bankB[b], gamB], [tmpB[kt_]])
                    dve(lambda e, t3=t3, g0=g0, g1=g1: e.tensor_tensor(out=t3, in0=t3, in1=Cg[:, g0:g1, :], op=ALU.add),
                        [tmpB[kt_], CgB], [tmpB[kt_]])
                    pool(lambda e, t3=t3, g0=g0, g1=g1, s=s: e.tensor_tensor(
                        out=aoutT[:, g0:g1, s * 128:(s + 1) * 128], in0=t3, in1=uT[:, g0:g1, s * 128:(s + 1) * 128], op=ALU.mult),
                        [tmpB[kt_]] + uB[g0:g1], [aoB[s]])
            ckpt('gmlp', [('aoutT', aoutT[:, :, :].rearrange('p a b -> p (a b)'), aoB), ('uT', uT.rearrange('p a b -> p (a b)'), uB)], at=(bq, tt))

            (wc1,), wc1B = wload("CQ1")
            for hh in range(4):
                wv_, wB_, c0 = (wc0, wa1B, hh * 128) if hh < 2 else (wc1, wc1B, (hh - 2) * 128)
                b = proj_fm(wv_, wB_, c0, hT, hTB)
                act(cqT[:, hh, :], banks[b][:, :], AF.Copy, [bankB[b]], [cqB[hh]])
            def mem_qk(hh):
                sc = []
                for mt in range(2):
                    b = balloc()
                    mm(banks[b][:, :], KmT[:, hh, mt * 128:(mt + 1) * 128], cqT[:, hh, :], True, True, [KmB, cqB[hh]], [bankB[b]])
                    sc.append(b)
                return sc

            scn = mem_qk(0)
            for hh in range(4):
                sc = scn
                if hh + 1 < 4:
                    scn = mem_qk(hh + 1)
                po = balloc(hold=True)
                pss = balloc(hold=True)
                for mt in range(2):
                    b = sc[mt]
                    pk = nxt("PT", NPT)
                    act(PT[pk][:, :], banks[b][:, :], AF.Exp, [bankB[b]], [PTB[pk]], scale=128.0 ** -0.5)
                    mm(banks[po][:, :], Vm[:, mt, hh * 128:(hh + 1) * 128], PT[pk][:, :], mt == 0, mt == 1, [VmB, PTB[pk]], [bankB[po]],
                       sig=True)
                    mm(banks[pss][:, :], onesb[:, :], PT[pk][:, :], mt == 0, mt == 1, [onesB, PTB[pk]], [bankB[pss]], sig=True)
                kt_ = nxt("tmp", 2)
                dve(lambda e, kt_=kt_, pss=pss: e.reciprocal(out=tmp[kt_][:, :], in_=banks[pss][:, :]), [bankB[pss]], [tmpB[kt_]])
                dve(lambda e, kt_=kt_, po=po, hh=hh: e.tensor_tensor(out=coutT[:, hh, :], in0=banks[po][:, :], in1=tmp[kt_][:, :],
                                                                      op=ALU.mult), [bankB[po], tmpB[kt_]], [coB[hh]])
                brelease(po)
                brelease(pss)
            ckpt('memattn', [('coutT', coutT.rearrange('p a b -> p (a b)'), coB)], at=(bq, tt))
            if (bq, tt) == (0, 0):
                cast_group("G", after=[coB[3]])

            for j in range(8):
                (wg0, wg1, wg2), wgB = wload("G%d" % j)
                (wba_, wbb_, wbc_), wbB = wload("B%d" % j)
                pg = []
                for wg_ in (wg0, wg1, wg2):
                    pg.append(proj_fm(wg_, wgB, 0, hT, hTB))
                pa = proj_fm(wba_, wbB, 0, aoutT, aoB, nk=6)
                pb_ = proj_fm(wbb_, wbB, 0, boutT, boB, nk=6)
                pc = proj_fm(wbc_, wbB, 0, coutT, coB, nk=4)
                for br in range(3):
                    act(sig[br][:, :], banks[pg[br]][:, :], AF.Sigmoid, [bankB[pg[br]]], [sigB[br]])
                dve(lambda e, pa=pa: e.tensor_tensor(out=tmp[0][:, :], in0=banks[pa][:, :], in1=sig[0][:, :], op=ALU.mult),
                    [bankB[pa], sigB[0]], [tmpB[0]])
                dve(lambda e, pb_=pb_: e.tensor_tensor(out=tmp[1][:, :], in0=banks[pb_][:, :], in1=sig[1][:, :], op=ALU.mult),
                    [bankB[pb_], sigB[1]], [tmpB[1]])
                dve(lambda e: e.tensor_tensor(out=tmp[0][:, :], in0=tmp[0][:, :], in1=tmp[1][:, :], op=ALU.add),
                    [tmpB[0], tmpB[1]], [tmpB[0]])
                dve(lambda e, pc=pc: e.tensor_tensor(out=tmp[1][:, :], in0=banks[pc][:, :], in1=sig[2][:, :], op=ALU.mult),
                    [bankB[pc], sigB[2]], [tmpB[1]])
                dve(lambda e, j=j: e.tensor_tensor(out=mergedT[:, j, :], in0=tmp[0][:, :], in1=tmp[1][:, :], op=ALU.add),
                    [tmpB[0], tmpB[1]], [mgB[j]])
            ckpt('merge', [('mergedT', mergedT[:, :, :].rearrange('p a b -> p (a b)'), mgB)], at=(bq, tt))

            tok0 = bq * SEQ + tt * T
            (wo0,), wo0B = wload("WO0")
            (wo1,), wo1B = wload("WO1")

            def oproj(s):
                pbs = []
                for (wo_, woB_) in ((wo0, wo0B), (wo1, wo1B)):
                    b = balloc(hold=True)
                    for kc in range(8):
                        mm(banks[b][:, :], mergedT[:, kc, s * 128:(s + 1) * 128], wo_[:, kc, :], kc == 0, kc == 7,
                           [woB_, mgB[kc]], [bankB[b]])
                    pbs.append(b)
                return pbs

            (wg0_, wu0_), wgu0B = wload("GU0")
            gu_groups = [(wg0_, 0), (wu0_, 0), (wg0_, 1), (wu0_, 1)]
            gub = []

            def gu0_slice(s):
                if not gub:
                    for _ in range(4):
                        gub.append(balloc(hold=True))
                for gi, (w_, q) in enumerate(gu_groups):
                    for kc in range(8):
                        mm(banks[gub[gi]][:, s * 128:(s + 1) * 128], w_[:, kc, q * 128:(q + 1) * 128], hT[:, kc, s * 128:(s + 1) * 128],
                           kc == 0, kc == 7, [wgu0B, hTB[s]], [bankB[gub[gi]]])

            ob = {0: oproj(0), 1: oproj(1)}
            pend = None
            for s in range(4):
                if s + 2 < 4:
                    ob[s + 2] = oproj(s + 2)
                pbs = ob.pop(s)
                kx = xs_next()
                P.dma("sp", xs[kx][:, :], x_d[tok0 + s * 128: tok0 + (s + 1) * 128, :], writes=[xsB[kx]], owner=xsB[kx])
                k = nxt("st", 2)
                s_, sB = st[k], stB[k]
                for c2 in range(2):
                    act(junk[:, :], banks[pbs[c2]][:, :], AF.Square, [bankB[pbs[c2]]], [junkB, sB], accum_out=s_[:, c2:c2 + 1])
                dve(lambda e, s_=s_: e.tensor_scalar(out=s_[:, 3:4], in0=s_[:, 0:1], scalar1=s_[:, 1:2], scalar2=1.0 / D,
                                                     op0=ALU.add, op1=ALU.mult), [sB], [sB])
                rstd_from(s_, sB, 3, 4, 5)
                for c2 in range(2):
                    kt_ = nxt("tmp", 2)
                    dve(lambda e, s_=s_, kt_=kt_, c2=c2, pbs=pbs: e.scalar_tensor_tensor(
                        out=tmp[kt_][:, :], in0=banks[pbs[c2]][:, :], scalar=s_[:, 5:6], in1=gpost_mix[:, c2 * 512:(c2 + 1) * 512],
                        op0=ALU.mult, op1=ALU.mult), [bankB[pbs[c2]], sB, gpmB], [tmpB[kt_]])
                    brelease(pbs[c2])
                    pool(lambda e, kt_=kt_, c2=c2, kx=kx: e.tensor_tensor(
                        out=xs[kx][:, c2 * 512:(c2 + 1) * 512], in0=xs[kx][:, c2 * 512:(c2 + 1) * 512], in1=tmp[kt_][:, :],
                        op=ALU.add), [tmpB[kt_], xsB[kx]], [xsB[kx]])
                P.dma("pool", out_d[tok0 + s * 128: tok0 + (s + 1) * 128, :], xs[kx][:, :], reads=[xsB[kx]], writes=[odB[s]],
                      owner=xsB[kx])
                kxn = norm_part(xs[kx][:, :], xsB[kx])
                if pend is not None:
                    transpose_part(pend[1], gffnT, gffnB, hT[:, :, pend[0] * 128:(pend[0] + 1) * 128], hTB[pend[0]])
                    gu0_slice(pend[0])
                pend = (s, kxn)
            transpose_part(pend[1], gffnT, gffnB, hT[:, :, pend[0] * 128:(pend[0] + 1) * 128], hTB[pend[0]])
            gu0_slice(pend[0])
            for q in range(2):
                pgt, pup = gub[2 * q], gub[2 * q + 1]
                ks = nxt("sig", 3)
                act(sig[ks][:, :], banks[pgt][:, :], AF.Silu, [bankB[pgt]], [sigB[ks]])
                dve(lambda e, ks=ks, pup=pup, q=q: e.tensor_tensor(out=hidT[:, q, :], in0=banks[pup][:, :], in1=sig[ks][:, :],
                                                                    op=ALU.mult), [bankB[pup], sigB[ks]], [hidB[q]])
                brelease(pgt)
                brelease(pup)
            ckpt('oproj', [('h2T', hT[:, :, :].rearrange('p a b -> p (a b)'), hTB)], at=(bq, tt))

        def tile_ffn(ti):
            bq, tt = tiles[ti]
            hT, hTB = hT2[ti % 2], hTB2[ti % 2]
            tok0 = bq * SEQ + tt * T
            nxt_new_seq = (ti + 1 < len(tiles)) and tiles[ti + 1][1] == 0
            s1k = {}
            for jj in range(1, 11):
                (wg_, wu_), wB_ = wload("GU%d" % jj)
                for q in range(2):
                    j = 2 * jj + q
                    pgt = proj_fm(wg_, wB_, q * 128, hT, hTB)
                    pup = proj_fm(wu_, wB_, q * 128, hT, hTB)
                    ks = nxt("sig", 3)
                    act(sig[ks][:, :], banks[pgt][:, :], AF.Silu, [bankB[pgt]], [sigB[ks]])
                    dve(lambda e, ks=ks, pup=pup, j=j: e.tensor_tensor(out=hidT[:, j, :], in0=banks[pup][:, :], in1=sig[ks][:, :],
                                                                        op=ALU.mult), [bankB[pup], sigB[ks]], [hidB[j]])
                if ti + 1 < len(tiles) and jj in (1, 3, 5, 7):
                    s1k[(jj - 1) // 2] = stage1_A(ti + 1, (jj - 1) // 2)
                if ti + 1 < len(tiles) and jj in (2, 4, 6, 8):
                    stage1_B(ti + 1, (jj - 2) // 2, s1k[(jj - 2) // 2])
            if nxt_new_seq:
                mem_stage(tiles[ti + 1][0])
            pd = [[None] * 4 for _ in range(2)]
            for c2 in range(2):
                for s in range(4):
                    pd[c2][s] = balloc(hold=True)
                for kg, (k0, kn) in enumerate(KG):
                    (wd_,), wdB_ = wload("D%d_%d" % (c2, kg))
                    for s in range(4):
                        for kl in range(kn):
                            kc = k0 + kl
                            mm(banks[pd[c2][s]][:, :], hidT[:, kc, s * 128:(s + 1) * 128], wd_[:, kl, :], kc == 0, kc == NFF - 1,
                               [wdB_, hidB[kc]], [bankB[pd[c2][s]]], sig=(kl == kn - 1))
            for s in range(4):
                kx = xs_next()
                P.dma("pool", xs[kx][:, :], out_d[tok0 + s * 128: tok0 + (s + 1) * 128, :], reads=[odB[s]], writes=[xsB[kx]],
                      owner=xsB[kx])
                k = nxt("st", 2)
                s_, sB = st[k], stB[k]
                for c2 in range(2):
                    act(junk[:, :], banks[pd[c2][s]][:, :], AF.Square, [bankB[pd[c2][s]]], [junkB, sB], accum_out=s_[:, c2:c2 + 1])
                dve(lambda e, s_=s_: e.tensor_scalar(out=s_[:, 3:4], in0=s_[:, 0:1], scalar1=s_[:, 1:2], scalar2=1.0 / D,
                                                     op0=ALU.add, op1=ALU.mult), [sB], [sB])
                rstd_from(s_, sB, 3, 4, 5)
                for c2 in range(2):
                    kt_ = nxt("tmp", 2)
                    dve(lambda e, s_=s_, kt_=kt_, c2=c2, s=s: e.scalar_tensor_tensor(
                        out=tmp[kt_][:, :], in0=banks[pd[c2][s]][:, :], scalar=s_[:, 5:6], in1=gpost_ffn[:, c2 * 512:(c2 + 1) * 512],
                        op0=ALU.mult, op1=ALU.mult), [bankB[pd[c2][s]], sB, gpfB], [tmpB[kt_]])
                    dve(lambda e, kt_=kt_, c2=c2, kx=kx: e.tensor_tensor(
                        out=xs[kx][:, c2 * 512:(c2 + 1) * 512], in0=xs[kx][:, c2 * 512:(c2 + 1) * 512], in1=tmp[kt_][:, :],
                        op=ALU.add), [tmpB[kt_], xsB[kx]], [xsB[kx]])
                    brelease(pd[c2][s])
                P.dma("pool", out_d[tok0 + s * 128: tok0 + (s + 1) * 128, :], xs[kx][:, :], reads=[xsB[kx]], writes=[odB[s]],
                      owner=xsB[kx])
            ckpt('ffn', [], at=(bq, tt))

        mem_stage(0)
        for s in range(4):
            stage1_B(0, s, stage1_A(0, s))
        for ti in range(len(tiles)):
            tile_front(ti)
            tile_ffn(ti)
    except _Stop:
        pass
    for o_ in dump_ops:
        P._wait('sp', o_.sem, o_.val)
    for b_ in xsB:
        for r in b_.dma_readers:
            P._wait("pool", r.sem, r.val)
    print("instr counts (signals):", P.cnt, "sems:", P.nsem)
    return nc


_NC_CACHE = {}


def kernel(**inputs):
    n = 8
    if "nc" not in _NC_CACHE:
        _NC_CACHE["nc"] = build()
    nc = _NC_CACHE["nc"]
    x = np.ascontiguousarray(inputs["x"], dtype=np.float32)
    mem = np.ascontiguousarray(inputs["mem"], dtype=np.float32)
    shared = {}
    for k in ("ln_mix_pre", "ln_mix_post", "ln_ffn_pre", "ln_ffn_post", "ln_mem", "ln_v_gain", "ln_v_bias"):
        shared[k] = np.ascontiguousarray(inputs[k], dtype=np.float32).reshape(1, -1)
    shared["w_in"] = np.ascontiguousarray(inputs["w_in"][0], dtype=np.float32)
    shared["w_spatial"] = np.ascontiguousarray(inputs["w_spatial"][0], dtype=np.float32)
    shared["b_spatial"] = np.ascontiguousarray(inputs["b_spatial"][0], dtype=np.float32).reshape(1, 768)
    shared["rel_bias"] = np.ascontiguousarray(inputs["rel_bias"], dtype=np.float32)
    for k in ("w_mem_kv", "w_branch_a", "w_branch_b", "w_branch_c", "w_out", "w_ffn_gate", "w_ffn_up", "w_ffn_down"):
        shared[k] = np.ascontiguousarray(inputs[k][0], dtype=np.float32)
    in_maps = []
    for c in range(n):
        m = dict(shared)
        m["x"] = x[2 * c:2 * c + 2].reshape(NSEQ * SEQ, D)
        m["mem"] = mem[2 * c:2 * c + 2].reshape(NSEQ * MEM, D)
        in_maps.append(m)
    res = run_bass_kernel_spmd(nc, in_maps, core_ids=list(range(n)))
    outs = [np.asarray(r["out"]).reshape(NSEQ, SEQ, D) for r in res.results]
    return np.concatenate(outs, axis=0).astype(np.float32, copy=False)
```

```python
import math
import numpy as np
import concourse.bass as bass
import concourse.mybir as mybir
from concourse.bass_utils import run_bass_kernel_spmd
from concourse.ap import AP

F32 = mybir.dt.float32
BF16 = mybir.dt.bfloat16
AF = mybir.ActivationFunctionType
ALU = mybir.AluOpType
AX = mybir.AxisListType

D = 1024
SEQ = 2048
NSEQ = 2
T = 512
NT = SEQ // T
MEM = 256
DFF = 2816
NFF = DFF // 128
EPS = 1e-6
NEGV = -30000.0
SLOT = 4096
NSLOT = 3
C_U, C_V, C_Q, C_K, C_VV, C_CQ, C_G = 0, 768, 1536, 2304, 3072, 3840, 4352


class Buf:
    def __init__(self, name):
        self.name = name
        self.last_w = None
        self.readers = {}
        self.dma_readers = []
        self.dma_sem = None
        self.dma_cnt = 0


class Op:
    __slots__ = ("eng", "sem", "val", "is_dma")

    def __init__(self, eng, is_dma=False):
        self.eng = eng
        self.sem = None
        self.val = None
        self.is_dma = is_dma


class Prog:
    def __init__(self, nc):
        self.nc = nc
        self.E = {"pe": nc.tensor, "act": nc.scalar, "dve": nc.vector, "pool": nc.gpsimd, "sp": nc.sync}
        self.sem = {e: nc.alloc_semaphore("eng_" + e) for e in self.E}
        self.cnt = {e: 0 for e in self.E}
        self.waited = {e: {} for e in self.E}
        self.pending = {e: [] for e in self.E}
        self.nsem = 5

    def _wait(self, eng, sem, val):
        w = self.waited[eng]
        k = id(sem)
        if w.get(k, 0) >= val:
            return
        self.E[eng].wait_ge(sem, val)
        w[k] = val

    def _dep(self, eng, op, skip_same):
        if op is None:
            return
        if (not op.is_dma) and op.eng == eng and eng == "pe":
            return
        assert op.val is not None, "dependency on unsignalled op"
        self._wait(eng, op.sem, op.val)

    def _deps(self, eng, reads, writes, strict=False):
        for b in reads:
            self._dep(eng, b.last_w, False)
        for b in writes:
            self._dep(eng, b.last_w, not strict)
            for r in b.readers.values():
                self._dep(eng, r, not strict)
            for r in b.dma_readers:
                self._dep(eng, r, False)

    def op(self, eng, fn, reads=(), writes=(), sig=True):
        self._deps(eng, reads, writes)
        ins = fn(self.E[eng])
        o = Op(eng)
        o.sem = self.sem[eng]
        if sig:
            self.cnt[eng] += 1
            ins.then_inc(self.sem[eng], 1)
            o.val = self.cnt[eng]
            for p in self.pending[eng]:
                p.val = o.val
            self.pending[eng] = []
        else:
            self.pending[eng].append(o)
        for b in writes:
            b.last_w = o
            b.readers = {}
            b.dma_readers = []
        for b in reads:
            b.readers[eng] = o
        return o

    def dma(self, q, out, in_, reads=(), writes=(), owner=None, **kw):
        self._deps(q, reads, writes, strict=True)
        if owner.dma_sem is None:
            owner.dma_sem = {}
            owner.dma_cnt = {}
        if q not in owner.dma_sem:
            owner.dma_sem[q] = self.nc.alloc_semaphore("dma_%s_%s" % (owner.name, q))
            owner.dma_cnt[q] = 0
            self.nsem += 1
        owner.dma_cnt[q] += 1
        self.E[q].dma_start(out=out, in_=in_, **kw).then_inc(owner.dma_sem[q], 16)
        o = Op(q, True)
        o.sem = owner.dma_sem[q]
        o.val = 16 * owner.dma_cnt[q]
        for b in writes:
            b.last_w = o
            b.readers = {}
            b.dma_readers = []
        for b in reads:
            b.dma_readers.append(o)
        return o


def t5_bucket_np(n):
    n = np.maximum(n, 0)
    nf = np.maximum(n, 1).astype(np.float32)
    large = 16 + (np.log(nf / np.float32(16)) / np.float32(math.log(8.0)) * np.float32(16)).astype(np.int32)
    large = np.minimum(large, 31)
    return np.where(n < 16, n, large)


def host_consts():
    ident = np.eye(128, dtype=np.float32)
    tril = np.tril(np.ones((128, 128), dtype=np.float32))
    oh = np.zeros((33, 384), dtype=np.float32)
    for j in range(384):
        n = j - 127
        if n < 0:
            oh[32, j] = 1.0
        else:
            b = int(t5_bucket_np(np.array([n]))[0])
            oh[b, j] += 1.0
            oh[31, j] -= 1.0
    e = np.zeros((96, 8, 128), dtype=np.float32)
    for jj in range(3):
        for n in range(8):
            e[jj * 32 + n, n, :] = 1.0
    return ident, tril, oh, e


class _Stop(Exception):
    pass


def build(stop=None, stop_at=(0, 0)):
    nc = bass.Bass("TRN2", target_bir_lowering=False)
    P = Prog(nc)
    dump_ops = []

    def ckpt(name, items, at=None):
        if stop != name or (at is not None and tuple(at) != tuple(stop_at)):
            return
        for (label, ap, bufs) in items:
            shp = list(ap.shape)
            dt_ = ap.dtype
            d = nc.dram_tensor("dbg_" + label, shp, dt_, kind="ExternalOutput").ap()
            ob = Buf("dbg_" + label)
            dump_ops.append(P.dma("sp", d, ap, reads=bufs, owner=ob))
        raise _Stop()

    def din(name, shape):
        return nc.dram_tensor(name, list(shape), F32, kind="ExternalInput").ap()

    x_d = din("x", [NSEQ * SEQ, D])
    mem_d = din("mem", [NSEQ * MEM, D])
    g_mix_pre = din("ln_mix_pre", [1, D])
    g_mix_post = din("ln_mix_post", [1, D])
    g_ffn_pre = din("ln_ffn_pre", [1, D])
    g_ffn_post = din("ln_ffn_post", [1, D])
    g_mem = din("ln_mem", [1, D])
    w_in = din("w_in", [D, 7424])
    lnv_g = din("ln_v_gain", [1, 768])
    lnv_b = din("ln_v_bias", [1, 768])
    w_sp = din("w_spatial", [6, 128, 128])
    b_sp = din("b_spatial", [1, 768])
    relb = din("rel_bias", [32, 12])
    w_mkv = din("w_mem_kv", [D, 1024])
    w_ba = din("w_branch_a", [768, D])
    w_bb = din("w_branch_b", [768, D])
    w_bc = din("w_branch_c", [512, D])
    w_o = din("w_out", [D, D])
    w_fg = din("w_ffn_gate", [D, DFF])
    w_fu = din("w_ffn_up", [D, DFF])
    w_fd = din("w_ffn_down", [DFF, D])
    out_d = nc.dram_tensor("out", [NSEQ * SEQ, D], F32, kind="ExternalOutput").ap()

    ident_h, tril_h, oh_h, e_h = host_consts()
    ident_d = nc.inline_tensor(ident_h, "c_ident").ap()
    tril_d = nc.inline_tensor(tril_h, "c_tril").ap()
    oh_d = nc.inline_tensor(oh_h, "c_oh").ap()
    e_d = nc.inline_tensor(e_h.reshape(96, 1024), "c_e").ap()

    chunks = {}
    order = []

    def defchunk(name, pieces, grp):
        chunks[name] = dict(pieces=pieces, grp=grp, idx=len(order))
        order.append(name)

    defchunk("MK", [(w_mkv, 0, 8, 0, 512)], "M")
    defchunk("MV", [(w_mkv, 0, 8, 512, 512)], "M")
    defchunk("Q0", [(w_in, 0, 8, C_Q, 512)], "A")
    defchunk("Q1K0", [(w_in, 0, 8, C_Q + 512, 256), (w_in, 0, 8, C_K, 256)], "A")
    defchunk("K1", [(w_in, 0, 8, C_K + 256, 512)], "A")
    defchunk("V0", [(w_in, 0, 8, C_VV, 512)], "B")
    defchunk("V1U0", [(w_in, 0, 8, C_VV + 512, 256), (w_in, 0, 8, C_U, 256)], "B")
    defchunk("U1", [(w_in, 0, 8, C_U + 256, 512)], "B")
    defchunk("AV0", [(w_in, 0, 8, C_V, 512)], "C")
    defchunk("AV1CQ0", [(w_in, 0, 8, C_V + 512, 256), (w_in, 0, 8, C_CQ, 256)], "C")
    defchunk("CQ1", [(w_in, 0, 8, C_CQ + 256, 256)], "C")
    for j in range(8):
        defchunk("G%d" % j, [(w_in, 0, 8, C_G + br * 1024 + j * 128, 128) for br in range(3)], "D")
        defchunk("B%d" % j, [(w_ba, 0, 6, j * 128, 128), (w_bb, 0, 6, j * 128, 128), (w_bc, 0, 4, j * 128, 128)], "D")
    defchunk("WO0", [(w_o, 0, 8, 0, 512)], "E")
    defchunk("WO1", [(w_o, 0, 8, 512, 512)], "E")
    for jj in range(11):
        defchunk("GU%d" % jj, [(w_fg, 0, 8, jj * 256, 256), (w_fu, 0, 8, jj * 256, 256)], "F")
    KG = [(0, 8), (8, 8), (16, 6)]
    for c2 in range(2):
        for kg, (k0, kn) in enumerate(KG):
            defchunk("D%d_%d" % (c2, kg), [(w_fd, k0 * 128, kn, c2 * 512, 512)], "G")

    wscr = nc.dram_tensor("wscr", [len(order), 128, SLOT], BF16).ap()
    grpB = {}
    for name in order:
        ch = chunks[name]
        off = 0
        views = []
        for (w, r0, kc, c0, cw) in ch["pieces"]:
            views.append((off, kc, cw))
            off += kc * cw
        assert off <= SLOT
        ch["views"] = views
        ch["used"] = off
        grpB[name] = Buf("cast_" + name)
        grpB[name].dma_sem = nc.alloc_semaphore("cast_" + name)
    cast_done = set()

    def cast_group(g, after=()):
        if g in cast_done:
            return
        cast_done.add(g)
        for b_ in after:
            P._dep("pool", b_.last_w, False)
        for name in order:
            ch = chunks[name]
            if ch["grp"] != g:
                continue
            n_ = 0
            for (w, r0, kc, c0, cw), (off, _, _) in zip(ch["pieces"], ch["views"]):
                src = w[r0:r0 + kc * 128, c0:c0 + cw].rearrange("(k p) c -> p k c", p=128)
                dst = wscr[ch["idx"], :, off:off + kc * cw].rearrange("p (k c) -> p k c", c=cw)
                nc.gpsimd.dma_start(out=dst, in_=src).then_inc(grpB[name].dma_sem, 16)
                n_ += 1
            o = Op("pool", True)
            o.sem = grpB[name].dma_sem
            o.val = 16 * n_
            grpB[name].last_w = o

    def sb(name, shape, dt):
        return nc.alloc_sbuf_tensor(name, list(shape), dt)

    ring = [sb("ring%d" % i, [128, SLOT], BF16) for i in range(NSLOT)]
    ringB = [Buf("ring%d" % i) for i in range(NSLOT)]
    ring_pos = [0]

    def wload(name):
        ch = chunks[name]
        i = ring_pos[0] % NSLOT
        ring_pos[0] += 1
        P.dma("sp", ring[i][:, 0:ch["used"]], wscr[ch["idx"], :, 0:ch["used"]],
              reads=[grpB[name]], writes=[ringB[i]], owner=ringB[i])
        vs = []
        for (off, kc, cw) in ch["views"]:
            vs.append(ring[i][:, off:off + kc * cw].rearrange("p (k c) -> p k c", c=cw))
        return vs, ringB[i]

    NXS = 2
    xs = [sb("xs%d" % i, [128, D], F32) for i in range(NXS)]
    xsB = [Buf("xs%d" % i) for i in range(NXS)]
    odB = [Buf("od%d" % s) for s in range(4)]
    hT2 = [sb("hT_%d" % i, [128, 8, T], BF16) for i in range(2)]
    hTB2 = [[Buf("hT%d_%d" % (i, s)) for s in range(4)] for i in range(2)]
    hT, hTB = hT2[0], hTB2[0]
    KT = sb("KT", [128, 6, SEQ], BF16)
    KTB = [[Buf("KT%d_%d" % (tt, j)) for j in range(6)] for tt in range(NT)]
    Vaug = sb("Vaug", [128, 16, 12, 65], BF16)
    VB = [Buf("V%d" % k) for k in range(16)]
    kmT = sb("kmT", [128, 6, 16], BF16)
    kmf = sb("kmf", [128, 6, 2], F32)
    kmB = [Buf("km%d" % tt) for tt in range(NT)]
    kmfB = Buf("kmf")
    region = sb("region", [128, NFF * T], BF16)
    hidT = region[:, :].rearrange("p (k t) -> p k t", t=T)
    hidB = [Buf("hid%d" % j) for j in range(NFF)]
    QT = region[:, 0:6 * T].rearrange("p (k t) -> p k t", t=T)
    QB = hidB[0:6]
    uT = region[:, 6 * T:12 * T].rearrange("p (k t) -> p k t", t=T)
    uB = hidB[6:12]
    cqT = region[:, 12 * T:16 * T].rearrange("p (k t) -> p k t", t=T)
    cqB = hidB[12:16]
    coutT = region[:, 16 * T:20 * T].rearrange("p (k t) -> p k t", t=T)
    coB = hidB[16:20]
    aoutT = sb("aoutT", [128, 6, T], BF16)
    aoB = [Buf("ao%d" % s) for s in range(4)]
    boutT = sb("boutT", [128, 6, T], BF16)
    boB = [Buf("bo%d" % s) for s in range(4)]
    mergedT = sb("mergedT", [128, 8, T], BF16)
    mgB = [Buf("mg%d" % j) for j in range(8)]
    identb = sb("identb", [128, 128], BF16); identB = Buf("identb")
    onesb = sb("onesb", [128, 128], BF16); onesB = Buf("onesb")
    Et = sb("Et", [96, 8, 128], BF16); EB = Buf("Et")
    wsT = sb("wsT", [128, 6, 128], BF16); wsTB = Buf("wsT")
    Cg = sb("Cg", [128, 6, 128], F32); CgB = Buf("Cg")
    gamT = sb("gamT", [128, 6], F32); gamB = Buf("gamT")
    betT = sb("betT", [128, 6], F32); betB = Buf("betT")
    gpreT = sb("gpreT", [128, 8], F32); gpreB = Buf("gpreT")
    gffnT = sb("gffnT", [128, 8], F32); gffnB = Buf("gffnT")
    gmemT = sb("gmemT", [128, 8], F32); gmemB = Buf("gmemT")
    gpost_mix = sb("gpost_mix", [128, D], F32); gpmB = Buf("gpm")
    gpost_ffn = sb("gpost_ffn", [128, D], F32); gpfB = Buf("gpf")
    Dt = sb("Dt", [128, 12, 2, 128], BF16); DtB = Buf("Dt")
    epsT = sb("epsT", [128, 1], F32); epsB = Buf("eps")
    KmT = sb("KmT", [128, 4, MEM], BF16); KmB = Buf("KmT")
    Vm = sb("Vm", [128, 2, 512], BF16); VmB = Buf("Vm")
    memT = sb("memT", [128, 8, MEM], BF16); memTB = [Buf("memT0"), Buf("memT1")]
    xn = [sb("xn%d" % i, [128, D], BF16) for i in range(2)]; xnB = [Buf("xn0"), Buf("xn1")]
    junk = sb("junk", [128, 512], BF16); junkB = Buf("junk")
    st = [sb("st%d" % i, [128, 8], F32) for i in range(2)]; stB = [Buf("st0"), Buf("st1")]
    gv = [sb("gv0", [128, 768], F32)]; gvB = [Buf("gv0")]
    vn = [sb("vn%d" % i, [128, 768], BF16) for i in range(2)]; vnB = [Buf("vn0"), Buf("vn1")]
    NPT = 4
    PT = [sb("PT%d" % i, [128, 512], BF16) for i in range(NPT)]; PTB = [Buf("PT%d" % i) for i in range(NPT)]
    sig = [sb("sig%d" % i, [128, 512], F32) for i in range(3)]; sigB = [Buf("sig%d" % i) for i in range(3)]
    tmp = [sb("tmp%d" % i, [128, 512], F32) for i in range(2)]; tmpB = [Buf("tmp0"), Buf("tmp1")]
    botok = sb("botok", [128, 4, 768], BF16); botokB = [Buf("botok%d" % i) for i in range(4)]
    gs = sb("gs", [128, 96], F32); gsB = Buf("gs")
    top8 = sb("top8", [128, 96], F32); top8B = Buf("top8")
    selb = sb("selb", [128, 576], BF16); selbB = Buf("selb")
    selbT = sb("selbT", [96, 6, T], BF16); selbTB = Buf("selbT")
    rinv = [sb("rinv%d" % i, [128, 4], F32) for i in range(2)]; rinvB = [Buf("rinv0"), Buf("rinv1")]
    print("SBUF bytes remaining/partition:", nc.sbuf_bytes_remaining)

    banks = [nc.alloc_psum_tensor("bank%d" % i, [128, 512], F32) for i in range(8)]
    bankB = [Buf("bank%d" % i) for i in range(8)]
    free_q = list(range(8))

    def _consumed(i):
        b_ = bankB[i]
        return b_.last_w is None or any(k != "pe" for k in b_.readers)

    def balloc(hold=False):
        for idx, i in enumerate(free_q):
            if _consumed(i):
                free_q.pop(idx)
                if not hold:
                    free_q.append(i)
                return i
        raise RuntimeError("PSUM schedule needs more than 8 live banks")

    def brelease(i):
        free_q.append(i)

    rot = {}

    def nxt(key, n):
        rot[key] = (rot.get(key, -1) + 1) % n
        return rot[key]

    def mm(out, lhsT, rhs, start, stop, reads, writes, sig=None, nogrp=False):
        if sig is None:
            sig = stop
        if nogrp:
            return P.op("pe", lambda e: e.matmul(out, lhsT=lhsT, rhs=rhs, start=start, stop=stop, skip_group_check=True),
                        reads=reads, writes=writes, sig=sig)
        return P.op("pe", lambda e: e.matmul(out, lhsT=lhsT, rhs=rhs, start=start, stop=stop),
                    reads=reads, writes=writes, sig=sig)

    def tr(out, in_, reads, writes, sig):
        return P.op("pe", lambda e: e.transpose(out, in_, identb[:, :]), reads=list(reads) + [identB], writes=writes, sig=sig)

    def act(out, in_, func, reads, writes, **kw):
        return P.op("act", lambda e: e.activation(out=out, in_=in_, func=func, **kw), reads=reads, writes=writes)

    def dve(fn, reads, writes):
        return P.op("dve", fn, reads=reads, writes=writes)

    def pool(fn, reads, writes):
        return P.op("pool", fn, reads=reads, writes=writes)

    def rstd_from(stt, stBuf, col_in, col_sd, col_out):
        act(stt[:, col_sd:col_sd + 1], stt[:, col_in:col_in + 1], AF.Sqrt, [stBuf, epsB], [stBuf], bias=epsT[:, 0:1], scale=1.0)
        dve(lambda e: e.reciprocal(out=stt[:, col_out:col_out + 1], in_=stt[:, col_sd:col_sd + 1]), [stBuf], [stBuf])

    def norm_part(src_ap, srcB):
        k = nxt("st", 2)
        s_, sB = st[k], stB[k]
        act(junk[:, :], src_ap[:, 0:512], AF.Square, [srcB], [junkB, sB], accum_out=s_[:, 0:1])
        act(junk[:, :], src_ap[:, 512:1024], AF.Square, [srcB], [junkB, sB], accum_out=s_[:, 1:2])
        dve(lambda e: e.tensor_scalar(out=s_[:, 2:3], in0=s_[:, 0:1], scalar1=s_[:, 1:2], scalar2=1.0 / D, op0=ALU.add, op1=ALU.mult),
            [sB], [sB])
        rstd_from(s_, sB, 2, 1, 3)
        kx = nxt("xn", 2)
        dve(lambda e: e.tensor_scalar(out=xn[kx][:, :], in0=src_ap, scalar1=s_[:, 3:4], scalar2=None, op0=ALU.mult),
            [srcB, sB], [xnB[kx]])
        return kx

    def transpose_part(kx, gT, gTB, dstT_ap3, dstB):
        b = balloc()
        pb16 = banks[b][:, :].bitcast(BF16)
        for kc in range(8):
            tr(pb16[:, kc * 128:(kc + 1) * 128], xn[kx][:, kc * 128:(kc + 1) * 128], [xnB[kx]], [bankB[b]], sig=(kc == 7))
        gb = AP(gT, 0, [[8, 128], [1, 8], [0, 128]])
        dve(lambda e: e.tensor_tensor(out=dstT_ap3, in0=pb16[:, :].rearrange("p (k t) -> p k t", t=128), in1=gb, op=ALU.mult),
            [bankB[b], gTB], [dstB])

    def norm_transpose(src_ap, srcB, gT, gTB, dstT_ap3, dstB):
        transpose_part(norm_part(src_ap, srcB), gT, gTB, dstT_ap3, dstB)

    def proj_fm(wv, wB, col0, rhsT, rhsBs, nk=8):
        b = balloc()
        for kc in range(nk):
            mm(banks[b][:, :], wv[:, kc, col0:col0 + 128], rhsT[:, kc, :], kc == 0, kc == nk - 1,
               [wB] + list(rhsBs), [bankB[b]])
        return b

    try:
        P.dma("pool", identb[:, :], ident_d, writes=[identB], owner=identB)
        P.dma("pool", Et[:, :, :].rearrange("p a b -> p (a b)"), e_d, writes=[EB], owner=EB)
        P.op("dve", lambda e: e.memset(onesb[:, :], 1.0), writes=[onesB])
        P.op("dve", lambda e: e.memset(epsT[:, :], EPS), writes=[epsB])
        P.op("dve", lambda e: e.memset(selb[:, :], 0.0), writes=[selbB])
        P.op("dve", lambda e: e.memset(kmT[:, :, :], 0.0), writes=kmB)
        P.op("dve", lambda e: e.memset(Vaug[:, :, :, 64:65], 1.0), writes=VB)
        for (tile_, tB, src, n) in ((gpreT, gpreB, g_mix_pre, 8), (gffnT, gffnB, g_ffn_pre, 8), (gmemT, gmemB, g_mem, 8),
                                    (gamT, gamB, lnv_g, 6), (betT, betB, lnv_b, 6)):
            P.dma("sp", tile_[:, :], src.rearrange("o (k p) -> p (o k)", p=128), writes=[tB], owner=tB,
                  allow_slow_non_contiguous=True)
        P.dma("sp", gpost_mix[:, :], g_mix_post.partition_broadcast(128), writes=[gpmB], owner=gpmB)
        P.dma("sp", gpost_ffn[:, :], g_ffn_post.partition_broadcast(128), writes=[gpfB], owner=gpfB)

        tmpc = region[:, :].bitcast(F32)
        wsl = tmpc[:, 3072:3840].rearrange("p (g s) -> p g s", s=128); wslB = Buf("wsl")
        trl = tmpc[:, 5376:5504]; trlB = Buf("trl")
        wsm = hT[:, 0:6, 0:128]; wsmB = Buf("wsm")
        bsB_t = tmpc[:, 3840:4608]; bsBB = Buf("bsB")
        P.dma("sp", wsl[:, :, :], w_sp.rearrange("g t s -> t g s"), writes=[wslB], owner=wslB)
        P.dma("sp", trl[:, :], tril_d, writes=[trlB], owner=trlB)
        P.dma("sp", bsB_t[:, :], b_sp.partition_broadcast(128), writes=[bsBB], owner=bsBB)
        for g in range(6):
            dve(lambda e, g=g: e.tensor_tensor(out=wsm[:, g, :], in0=wsl[:, g, :], in1=trl[:, :], op=ALU.mult), [wslB, trlB], [wsmB])
        b = balloc()
        pb16 = banks[b][:, :].bitcast(BF16)
        for g in range(6):
            tr(pb16[:, g * 128:(g + 1) * 128], wsm[:, g, :], [wsmB], [bankB[b]], sig=(g == 5))
        dve(lambda e: e.tensor_copy(out=wsT[:, :, :].rearrange("p g t -> p (g t)"), in_=pb16[:, 0:768]), [bankB[b]], [wsTB])
        for (g0, g1) in ((0, 4), (4, 6)):
            b = balloc()
            for g in range(g0, g1):
                mm(banks[b][:, (g - g0) * 128:(g - g0 + 1) * 128], onesb[:, :], wsT[:, g, :], True, True, [onesB, wsTB], [bankB[b]],
                   sig=(g == g1 - 1))
            for g in range(g0, g1):
                dve(lambda e, g=g, b=b, g0=g0: e.scalar_tensor_tensor(
                    out=Cg[:, g, :], in0=banks[b][:, (g - g0) * 128:(g - g0 + 1) * 128], scalar=betT[:, g:g + 1],
                    in1=bsB_t[:, g * 128:(g + 1) * 128], op0=ALU.mult, op1=ALU.add), [bankB[b], betB, bsBB], [CgB])

        P.op("dve", lambda e: e.memset(rinv[0][:, 0:1], 0.0), reads=[wslB, trlB, bsBB, wsmB],
             writes=hidB + [hTB[0], rinvB[0]])
        for g_ in "MABCDEFG":
            cast_group(g_)
        rba = sb("rba", [33, 12], F32); rbaB = Buf("rba")
        bo32 = boutT[:, :, :].rearrange("p a b -> p (a b)").bitcast(F32)
        ohs = bo32[0:33, 0:384]
        Fsb = bo32[0:12, 384:768]
        D32a = mergedT[:, :, :].rearrange("p a b -> p (a b)").bitcast(F32)[:, 0:1536].rearrange("p (a c) -> p a c", c=128)
        D32b = aoutT[:, :, :].rearrange("p a b -> p (a b)").bitcast(F32)[:, 0:1536].rearrange("p (a c) -> p a c", c=128)
        Mscr = nc.dram_tensor("Mscr", [12, 128 * 384], F32)
        MB = Buf("Mscr")
        P.op("dve", lambda e: e.memset(rba[:, :], NEGV), writes=[rbaB])
        P.dma("sp", rba[0:32, :], relb, writes=[rbaB], owner=rbaB)
        P.dma("sp", ohs, oh_d, writes=boB, owner=boB[0])

        def dchain_finish():
            b = balloc()
            mm(banks[b][0:12, 0:384], rba[0:33, 0:12], ohs, True, True, [rbaB] + boB, [bankB[b]])
            dve(lambda e: e.tensor_copy(out=Fsb, in_=banks[b][0:12, 0:384]), [bankB[b]], boB)
            srcF = Fsb.unsqueeze(1).broadcast_to([12, 128, 384])
            P.dma("sp", Mscr.ap().rearrange("h (k j) -> h k j", j=384), srcF, reads=boB, writes=[MB], owner=MB)
            P.dma("sp", D32a, AP(Mscr, 127, [[383, 128], [128 * 384, 12], [1, 128]]), reads=[MB], writes=mgB, owner=mgB[0])
            P.dma("sp", D32b, AP(Mscr, 127 + 128, [[383, 128], [128 * 384, 12], [1, 128]]), reads=[MB], writes=aoB, owner=aoB[0])
            dve(lambda e: e.tensor_copy(out=Dt[:, :, 0, :], in_=D32a), mgB, [DtB])
            dve(lambda e: e.tensor_copy(out=Dt[:, :, 1, :], in_=D32b), aoB, [DtB])

        ckpt('consts', [('wsT', wsT[:, :, :].rearrange('p g t -> p (g t)'), [wsTB]), ('Cg', Cg[:, :, :].rearrange('p g t -> p (g t)'), [CgB]),
                        ('gpreT', gpreT[:, :], [gpreB]),
                        ('gpm', gpost_mix[:, :], [gpmB]), ('Et', Et[:, :, :].rearrange('p a b -> p (a b)'), [EB]), ('gamT', gamT[:, :], [gamB])])
        tiles = [(bq, tt) for bq in range(NSEQ) for tt in range(NT)]

        def xs_next():
            return nxt("xs", NXS)

        def mem_stage(bq):
            for mt in range(2):
                r0 = bq * MEM + mt * 128
                k = xs_next()
                P.dma("sp", xs[k][:, :], mem_d[r0:r0 + 128, :], writes=[xsB[k]], owner=xsB[k])
                norm_transpose(xs[k][:, :], xsB[k], gmemT, gmemB, memT[:, :, mt * 128:(mt + 1) * 128], memTB[mt])
            (wk,), wkB = wload("MK")
            for hh in range(4):
                b = balloc()
                for kc in range(8):
                    mm(banks[b][:, 0:MEM], wk[:, kc, hh * 128:(hh + 1) * 128], memT[:, kc, :], kc == 0, kc == 7,
                       [wkB] + memTB, [bankB[b]])
                act(KmT[:, hh, :], banks[b][:, 0:MEM], AF.Copy, [bankB[b]], [KmB])
            (wv,), wvB = wload("MV")
            for mt in range(2):
                b = balloc()
                for kc in range(8):
                    mm(banks[b][:, :], memT[:, kc, mt * 128:(mt + 1) * 128], wv[:, kc, :], kc == 0, kc == 7,
                       [wvB, memTB[mt]], [bankB[b]])
                dve(lambda e, b=b, mt=mt: e.tensor_copy(out=Vm[:, mt, :], in_=banks[b][:, :]), [bankB[b]], [VmB])

        def stage1_A(ti, s):
            bq, tt = tiles[ti]
            tok0 = bq * SEQ + tt * T
            k = xs_next()
            P.dma("sp", xs[k][:, :], x_d[tok0 + s * 128: tok0 + (s + 1) * 128, :], writes=[xsB[k]], owner=xsB[k])
            return norm_part(xs[k][:, :], xsB[k])

        def stage1_B(ti, s, kx):
            transpose_part(kx, gpreT, gpreB, hT2[ti % 2][:, :, s * 128:(s + 1) * 128], hTB2[ti % 2][s])

        def tile_front(ti):
            bq, tt = tiles[ti]
            hT, hTB = hT2[ti % 2], hTB2[ti % 2]
            if (bq, tt) == (0, 0):
                cast_group("C", after=[hTB[0]])
            (wq0,), wq0B = wload("Q0")
            (wq1, wk0), wq1B = wload("Q1K0")
            for j in range(6):
                wv_, wB_, c0 = (wq0, wq0B, j * 128) if j < 4 else (wq1, wq1B, (j - 4) * 128)
                b = proj_fm(wv_, wB_, c0, hT, hTB)
                dve(lambda e, b=b, j=j: e.tensor_scalar(out=QT[:, j, :], in0=banks[b][:, :], scalar1=0.125, scalar2=None,
                                                        op0=ALU.mult), [bankB[b]], [QB[j]])
            (wk1,), wk1B = wload("K1")
            for j in range(6):
                wv_, wB_, c0 = (wk0, wq1B, j * 128) if j < 2 else (wk1, wk1B, (j - 2) * 128)
                b = proj_fm(wv_, wB_, c0, hT, hTB)
                act(KT[:, j, tt * T:(tt + 1) * T], banks[b][:, :], AF.Copy, [bankB[b]], [KTB[tt][j]])
            dve(lambda e: e.reduce_sum(out=kmf[:, :, :], in_=KT[:, :, tt * T:(tt + 1) * T].rearrange("p k (b t) -> p k b t", b=2),
                                       axis=AX.X), KTB[tt], [kmfB])
            dve(lambda e: e.tensor_scalar(out=kmT[0:64, :, 2 * tt:2 * tt + 2], in0=kmf[0:64, :, :], scalar1=1.0 / 256,
                                          scalar2=None, op0=ALU.mult), [kmfB], [kmB[tt]])
            dve(lambda e: e.tensor_scalar(out=kmT[64:128, :, 8 + 2 * tt:8 + 2 * tt + 2], in0=kmf[64:128, :, :], scalar1=1.0 / 256,
                                          scalar2=None, op0=ALU.mult), [kmfB], [kmB[tt]])
            if (bq, tt) == (0, 0):
                dchain_finish()
            (wv0,), wv0B = wload("V0")
            (wv1, wu0), wv1B = wload("V1U0")

            def vproj_kv(s):
                kt = tt * 4 + s
                bA = balloc()
                for kc in range(8):
                    mm(banks[bA][:, :], hT[:, kc, s * 128:(s + 1) * 128], wv0[:, kc, :], kc == 0, kc == 7,
                       [wv0B, hTB[s]], [bankB[bA]])
                bB = balloc()
                for kc in range(8):
                    mm(banks[bB][:, 0:256], hT[:, kc, s * 128:(s + 1) * 128], wv1[:, kc, :], kc == 0, kc == 7,
                       [wv1B, hTB[s]], [bankB[bB]])
                act(Vaug[:, kt, 0:8, 0:64], banks[bA][:, :].rearrange("p (h d) -> p h d", d=64), AF.Copy, [bankB[bA]], [VB[kt]])
                act(Vaug[:, kt, 8:12, 0:64], banks[bB][:, 0:256].rearrange("p (h d) -> p h d", d=64), AF.Copy, [bankB[bB]],
                    [VB[kt]])

            def sel_gate(s):
                b = balloc()
                for j in range(6):
                    mm(banks[b][:, j * 16:(j + 1) * 16], QT[:, j, s * 128:(s + 1) * 128], kmT[:, j, 0:16], True, True,
                       [QB[j]] + kmB, [bankB[b]], sig=(j == 5))
                return b

            def sel_chain(s, b):
                c = 2 * tt + s // 2
                dve(lambda e, b=b: e.tensor_copy(out=gs[:, :], in_=banks[b][:, 0:96]), [bankB[b]], [gsB])
                dve(lambda e, c=c: e.memset(gs[:, :].rearrange("p (h n) -> p h n", n=8)[:, :, c:8], -1e30), [], [gsB])
                for h in range(12):
                    dve(lambda e, h=h: e.max(out=top8[:, h * 8:(h + 1) * 8], in_=gs[:, h * 8:(h + 1) * 8]), [gsB], [top8B])
                thr = AP(top8, 2, [[96, 128], [8, 12], [0, 8]])
                dve(lambda e: e.tensor_tensor(out=gs[:, :].rearrange("p (h n) -> p h n", n=8),
                                              in0=gs[:, :].rearrange("p (h n) -> p h n", n=8), in1=thr, op=ALU.is_ge),
                    [gsB, top8B], [gsB])
                sel_e_o = AP(selb, 0, [[576, 128], [96, 3], [32, 2], [1, 8]])
                sel_e_i = AP(gs, 0, [[96, 128], [32, 3], [16, 2], [1, 8]])
                dve(lambda e: e.tensor_scalar(out=sel_e_o, in0=sel_e_i, scalar1=1.0, scalar2=-NEGV, op0=ALU.subtract,
                                              op1=ALU.mult), [gsB], [selbB])
                sel_o_o = AP(selb, 64, [[576, 128], [96, 6], [1, 8]])
                sel_o_i = AP(gs, 8, [[96, 128], [16, 6], [1, 8]])
                dve(lambda e: e.tensor_scalar(out=sel_o_o, in0=sel_o_i, scalar1=1.0, scalar2=-NEGV, op0=ALU.subtract,
                                              op1=ALU.mult), [gsB], [selbB])

            def sel_trans(s):
                for half in range(2):
                    b2 = balloc()
                    p16 = banks[b2][:, :].bitcast(BF16)
                    for g in range(3):
                        gg = half * 3 + g
                        tr(p16[0:96, g * 128:(g + 1) * 128], selb[:, gg * 96:(gg + 1) * 96], [selbB], [bankB[b2]], sig=(g == 2))
                    act(selbT[:, half * 3:half * 3 + 3, s * 128:(s + 1) * 128],
                        p16[0:96, 0:384].rearrange("p (g q) -> p g q", q=128), AF.Copy, [bankB[b2]], [selbTB])

            if tt >= 2:
                vproj_kv(0)
                gb_ = sel_gate(0)
                for s in range(4):
                    if s + 1 < 4:
                        vproj_kv(s + 1)
                    sel_chain(s, gb_)
                    if s + 1 < 4:
                        gb_ = sel_gate(s + 1)
                    sel_trans(s)
            else:
                for s in range(4):
                    vproj_kv(s)
            ckpt('kvq', [('KT', KT[:, :, :].rearrange('p a b -> p (a b)'), [x_ for r_ in KTB for x_ in r_]), ('QT', QT.rearrange('p a b -> p (a b)'), QB),
                         ('Vaug', Vaug[:, :, :, :].rearrange('p a b c -> p (a b c)'), VB), ('kmT', kmT[:, :, :].rearrange('p a b -> p (a b)'), kmB)], at=(bq, tt))
            if (bq, tt) == (0, 0):
                cast_group("D", after=[QB[5]])

            nkt = 4 * tt + 4
            jobs = [(j, kt) for j in range(6) for kt in range(nkt)]
            jstate = {}
            pvbank = {}

            def qk(i):
                j, kt = jobs[i]
                a = kt - 4 * tt
                qlo = max(a, 0) * 128
                bb = [balloc(), balloc()]
                ex = [[], []]
                for hp in range(2):
                    h = 2 * j + hp
                    b = bb[hp]
                    if hp == 0:
                        mb, g3 = (j % 2) * 32, j // 2
                    else:
                        mb, g3 = 64, j
                    if tt >= 2:
                        n = kt // 2
                        if a < 0:
                            ex[hp].append((banks[b][:, 0:512], Et[mb:mb + 8, n, :], selbT[mb:mb + 8, g3, 0:512], [EB, selbTB]))
                        elif a < 2:
                            ex[hp].append((banks[b][:, 256:512], Et[mb:mb + 8, n, :], selbT[mb:mb + 8, g3, 256:512], [EB, selbTB]))
                bias = [[], []]
                for hp in range(2):
                    h = 2 * j + hp
                    b = bb[hp]
                    if a == -1:
                        bias[hp].append((banks[b][:, 0:128], identb[:, :], Dt[:, h, 1, :], [identB, DtB]))
                    if a >= 0:
                        if a < 3:
                            bias[hp].append((banks[b][:, a * 128:(a + 2) * 128], identb[:, :],
                                             Dt[:, h, :, :].rearrange("p a b -> p (a b)"), [identB, DtB]))
                        else:
                            bias[hp].append((banks[b][:, a * 128:(a + 1) * 128], identb[:, :], Dt[:, h, 0, :], [identB, DtB]))
                for hp in range(2):
                    ps = slice(hp * 64, (hp + 1) * 64)
                    nx = len(ex[hp]) + len(bias[hp])
                    mm(banks[bb[hp]][:, qlo:512], KT[ps, j, kt * 128:(kt + 1) * 128], QT[ps, j, qlo:512], True, nx == 0,
                       [KTB[kt // 4][j], QB[j]], [bankB[bb[hp]]])
                for hp in range(2):
                    for xi, (o_, l_, r_, rb_) in enumerate(ex[hp]):
                        mm(o_, l_, r_, False, len(bias[hp]) == 0 and xi == len(ex[hp]) - 1, rb_, [bankB[bb[hp]]])
                for hp in range(2):
                    for xi, (o_, l_, r_, rb_) in enumerate(bias[hp]):
                        mm(o_, l_, r_, False, xi == len(bias[hp]) - 1, rb_, [bankB[bb[hp]]])
                jstate[i] = (bb, qlo)

            qk(0)
            for i in range(len(jobs)):
                if i + 1 < len(jobs):
                    qk(i + 1)
                j, kt = jobs[i]
                bb, qlo = jstate.pop(i)
                pks = []
                for hp in range(2):
                    pk = nxt("PT", NPT)
                    pks.append(pk)
                    act(PT[pk][:, qlo:512], banks[bb[hp]][:, qlo:512], AF.Exp, [bankB[bb[hp]]], [PTB[pk]])
                if kt == 0:
                    pvbank[j] = [balloc(hold=True), balloc(hold=True)]
                s_lo = qlo // 128
                for hp in range(2):
                    h = 2 * j + hp
                    pvb = pvbank[j][hp]
                    pv = banks[pvb]
                    pk = pks[hp]
                    for s in range(s_lo, 4):
                        mm(pv[:, s * 65:(s + 1) * 65], PT[pk][:, s * 128:(s + 1) * 128], Vaug[:, kt, h, 0:65],
                           kt == 0 and s == 0, kt == 4 * tt + s, [PTB[pk], VB[kt]], [bankB[pvb]], sig=(s == 3), nogrp=True)
                if kt == nkt - 1:
                    for hp in range(2):
                        h = 2 * j + hp
                        pvb = pvbank[j][hp]
                        pv = banks[pvb]
                        rk = nxt("rinv", 2)
                        pv3 = pv[:, 0:260].rearrange("p (q d) -> p q d", d=65)
                        dve(lambda e, rk=rk, pv3=pv3: e.reciprocal(out=rinv[rk][:, :].rearrange("p (q o) -> p q o", o=1),
                                                                   in_=pv3[:, :, 64:65]), [bankB[pvb]], [rinvB[rk]])
                        rb3 = AP(rinv[rk], 0, [[4, 128], [1, 4], [0, 64]])
                        dve(lambda e, pv3=pv3, rb3=rb3, h=h: e.tensor_tensor(
                            out=botok[:, :, h * 64:(h + 1) * 64], in0=pv3[:, :, 0:64], in1=rb3, op=ALU.mult),
                            [bankB[pvb], rinvB[rk]], botokB)
                        brelease(pvb)
            for s in range(4):
                b = balloc()
                p16 = banks[b][:, :].bitcast(BF16)
                for jc in range(6):
                    tr(p16[:, jc * 128:(jc + 1) * 128], botok[:, s, jc * 128:(jc + 1) * 128], [botokB[s]], [bankB[b]], sig=(jc == 5))
                act(boutT[:, :, s * 128:(s + 1) * 128], p16[:, 0:768].rearrange("p (k t) -> p k t", t=128), AF.Copy,
                    [bankB[b]], [boB[s]])
            ckpt('attn', [('boutT', boutT[:, :, :].rearrange('p a b -> p (a b)'), boB)], at=(bq, tt))
            if (bq, tt) == (0, 0):
                cast_group("E", after=[boB[0]])
                cast_group("F", after=[boB[3]])

            (wu1,), wu1B = wload("U1")
            for j in range(6):
                wv_, wB_, c0 = (wu0, wv1B, j * 128) if j < 2 else (wu1, wu1B, (j - 2) * 128)
                b = proj_fm(wv_, wB_, c0, hT, hTB)
                act(uT[:, j, :], banks[b][:, :], AF.Gelu_apprx_tanh, [bankB[b]], [uB[j]])
            (wa0,), wa0B = wload("AV0")
            (wa1, wc0), wa1B = wload("AV1CQ0")

            def vproj(s):
                bA = balloc()
                for kc in range(8):
                    mm(banks[bA][:, :], hT[:, kc, s * 128:(s + 1) * 128], wa0[:, kc, :], kc == 0, kc == 7, [wa0B, hTB[s]], [bankB[bA]])
                bB = balloc()
                for kc in range(8):
                    mm(banks[bB][:, 0:256], hT[:, kc, s * 128:(s + 1) * 128], wa1[:, kc, :], kc == 0, kc == 7, [wa1B, hTB[s]],
                       [bankB[bB]])
                return bA, bB

            vb = {0: vproj(0), 1: vproj(1)}
            for s in range(4):
                if s + 2 < 4:
                    vb[s + 2] = vproj(s + 2)
                bA, bB = vb.pop(s)
                k = nxt("st", 2)
                s_, sB = st[k], stB[k]
                kg_ = 0
                gv_, gvB_ = gv[kg_], gvB[kg_]
                act(gv_[:, 0:512], banks[bA][:, :], AF.Gelu_apprx_tanh, [bankB[bA]], [gvB_, sB], accum_out=s_[:, 0:1])
                act(gv_[:, 512:768], banks[bB][:, 0:256], AF.Gelu_apprx_tanh, [bankB[bB]], [gvB_, sB], accum_out=s_[:, 1:2])
                act(junk[:, 0:512], gv_[:, 0:512], AF.Square, [gvB_], [junkB, sB], accum_out=s_[:, 2:3])
                act(junk[:, 0:256], gv_[:, 512:768], AF.Square, [gvB_], [junkB, sB], accum_out=s_[:, 6:7])
                dve(lambda e, s_=s_: e.tensor_scalar(out=s_[:, 3:4], in0=s_[:, 0:1], scalar1=s_[:, 1:2], scalar2=1.0 / 768,
                                                     op0=ALU.add, op1=ALU.mult), [sB], [sB])
                dve(lambda e, s_=s_: e.tensor_scalar(out=s_[:, 2:3], in0=s_[:, 2:3], scalar1=s_[:, 6:7], scalar2=1.0 / 768,
                                                     op0=ALU.add, op1=ALU.mult), [sB], [sB])
                dve(lambda e, s_=s_: e.tensor_tensor(out=s_[:, 4:5], in0=s_[:, 3:4], in1=s_[:, 3:4], op=ALU.mult), [sB], [sB])
                dve(lambda e, s_=s_: e.tensor_tensor(out=s_[:, 5:6], in0=s_[:, 2:3], in1=s_[:, 4:5], op=ALU.subtract), [sB], [sB])
                rstd_from(s_, sB, 5, 6, 7)
                dve(lambda e, s_=s_: e.scalar_tensor_tensor(out=s_[:, 4:5], in0=s_[:, 3:4], scalar=-1.0, in1=s_[:, 7:8],
                                                            op0=ALU.mult, op1=ALU.mult), [sB], [sB])
                kv = nxt("vn", 2)
                dve(lambda e, s_=s_, kv=kv, gv_=gv_: e.tensor_scalar(out=vn[kv][:, :], in0=gv_[:, :], scalar1=s_[:, 7:8],
                                                                     scalar2=s_[:, 4:5], op0=ALU.mult, op1=ALU.add),
                    [gvB_, sB], [vnB[kv]])
                for (g0, g1) in ((0, 4), (4, 6)):
                    b = balloc()
                    ng = g1 - g0
                    for g in range(g0, g1):
                        mm(banks[b][:, (g - g0) * 128:(g - g0 + 1) * 128], vn[kv][:, g * 128:(g + 1) * 128], wsT[:, g, :], True, True,
                           [vnB[kv], wsTB], [bankB[b]], sig=(g == g1 - 1))
                    kt_ = nxt("tmp", 2)
                    t3 = tmp[kt_][:, 0:ng * 128].rearrange("p (g t) -> p g t", t=128)
                    pm3 = banks[b][:, 0:ng * 128].rearrange("p (g t) -> p g t", t=128)
                    gb = AP(gamT, g0, [[6, 128], [1, ng], [0, 128]])
                    dve(lambda e, t3=t3, pm3=pm3, gb=gb: e.tensor_tensor(out=t3, in0=pm3, in1=gb, op=ALU.mult),
                        [bankB[b], gamB], [tmpB[kt_]])
                    dve(lambda e, t3=t3, g0=g0, g1=g1: e.tensor_tensor(out=t3, in0=t3, in1=Cg[:, g0:g1, :], op=ALU.add),
                        [tmpB[kt_], CgB], [tmpB[kt_]])
                    pool(lambda e, t3=t3, g0=g0, g1=g1, s=s: e.tensor_tensor(
                        out=aoutT[:, g0:g1, s * 128:(s + 1) * 128], in0=t3, in1=uT[:, g0:g1, s * 128:(s + 1) * 128], op=ALU.mult),
                        [tmpB[kt_]] + uB[g0:g1], [aoB[s]])
            ckpt('gmlp', [('aoutT', aoutT[:, :, :].rearrange('p a b -> p (a b)'), aoB), ('uT', uT.rearrange('p a b -> p (a b)'), uB)], at=(bq, tt))

            (wc1,), wc1B = wload("CQ1")
            for hh in range(4):
                wv_, wB_, c0 = (wc0, wa1B, hh * 128) if hh < 2 else (wc1, wc1B, (hh - 2) * 128)
                b = proj_fm(wv_, wB_, c0, hT, hTB)
                act(cqT[:, hh, :], banks[b][:, :], AF.Copy, [bankB[b]], [cqB[hh]])
            def mem_qk(hh):
                sc = []
                for mt in range(2):
                    b = balloc()
                    mm(banks[b][:, :], KmT[:, hh, mt * 128:(mt + 1) * 128], cqT[:, hh, :], True, True, [KmB, cqB[hh]], [bankB[b]])
                    sc.append(b)
                return sc

            scn = mem_qk(0)
            for hh in range(4):
                sc = scn
                if hh + 1 < 4:
                    scn = mem_qk(hh + 1)
                po = balloc(hold=True)
                pss = balloc(hold=True)
                for mt in range(2):
                    b = sc[mt]
                    pk = nxt("PT", NPT)
                    act(PT[pk][:, :], banks[b][:, :], AF.Exp, [bankB[b]], [PTB[pk]], scale=128.0 ** -0.5)
                    mm(banks[po][:, :], Vm[:, mt, hh * 128:(hh + 1) * 128], PT[pk][:, :], mt == 0, mt == 1, [VmB, PTB[pk]], [bankB[po]],
                       sig=True)
                    mm(banks[pss][:, :], onesb[:, :], PT[pk][:, :], mt == 0, mt == 1, [onesB, PTB[pk]], [bankB[pss]], sig=True)
                kt_ = nxt("tmp", 2)
                dve(lambda e, kt_=kt_, pss=pss: e.reciprocal(out=tmp[kt_][:, :], in_=banks[pss][:, :]), [bankB[pss]], [tmpB[kt_]])
                dve(lambda e, kt_=kt_, po=po, hh=hh: e.tensor_tensor(out=coutT[:, hh, :], in0=banks[po][:, :], in1=tmp[kt_][:, :],
                                                                      op=ALU.mult), [bankB[po], tmpB[kt_]], [coB[hh]])
                brelease(po)
                brelease(pss)
            ckpt('memattn', [('coutT', coutT.rearrange('p a b -> p (a b)'), coB)], at=(bq, tt))
            if (bq, tt) == (0, 0):
                cast_group("G", after=[coB[3]])

            for j in range(8):
                (wg0, wg1, wg2), wgB = wload("G%d" % j)
                (wba_, wbb_, wbc_), wbB = wload("B%d" % j)
                pg = []
                for wg_ in (wg0, wg1, wg2):
                    pg.append(proj_fm(wg_, wgB, 0, hT, hTB))
                pa = proj_fm(wba_, wbB, 0, aoutT, aoB, nk=6)
                pb_ = proj_fm(wbb_, wbB, 0, boutT, boB, nk=6)
                pc = proj_fm(wbc_, wbB, 0, coutT, coB, nk=4)
                for br in range(3):
                    act(sig[br][:, :], banks[pg[br]][:, :], AF.Sigmoid, [bankB[pg[br]]], [sigB[br]])
                dve(lambda e, pa=pa: e.tensor_tensor(out=tmp[0][:, :], in0=banks[pa][:, :], in1=sig[0][:, :], op=ALU.mult),
                    [bankB[pa], sigB[0]], [tmpB[0]])
                dve(lambda e, pb_=pb_: e.tensor_tensor(out=tmp[1][:, :], in0=banks[pb_][:, :], in1=sig[1][:, :], op=ALU.mult),
                    [bankB[pb_], sigB[1]], [tmpB[1]])
                dve(lambda e: e.tensor_tensor(out=tmp[0][:, :], in0=tmp[0][:, :], in1=tmp[1][:, :], op=ALU.add),
                    [tmpB[0], tmpB[1]], [tmpB[0]])
                dve(lambda e, pc=pc: e.tensor_tensor(out=tmp[1][:, :], in0=banks[pc][:, :], in1=sig[2][:, :], op=ALU.mult),
                    [bankB[pc], sigB[2]], [tmpB[1]])
                dve(lambda e, j=j: e.tensor_tensor(out=mergedT[:, j, :], in0=tmp[0][:, :], in1=tmp[1][:, :], op=ALU.add),
                    [tmpB[0], tmpB[1]], [mgB[j]])
            ckpt('merge', [('mergedT', mergedT[:, :, :].rearrange('p a b -> p (a b)'), mgB)], at=(bq, tt))

            tok0 = bq * SEQ + tt * T
            (wo0,), wo0B = wload("WO0")
            (wo1,), wo1B = wload("WO1")

            def oproj(s):
                pbs = []
                for (wo_, woB_) in ((wo0, wo0B), (wo1, wo1B)):
                    b = balloc(hold=True)
                    for kc in range(8):
                        mm(banks[b][:, :], mergedT[:, kc, s * 128:(s + 1) * 128], wo_[:, kc, :], kc == 0, kc == 7,
                           [woB_, mgB[kc]], [bankB[b]])
                    pbs.append(b)
                return pbs

            (wg0_, wu0_), wgu0B = wload("GU0")
            gu_groups = [(wg0_, 0), (wu0_, 0), (wg0_, 1), (wu0_, 1)]
            gub = []

            def gu0_slice(s):
                if not gub:
                    for _ in range(4):
                        gub.append(balloc(hold=True))
                for gi, (w_, q) in enumerate(gu_groups):
                    for kc in range(8):
                        mm(banks[gub[gi]][:, s * 128:(s + 1) * 128], w_[:, kc, q * 128:(q + 1) * 128], hT[:, kc, s * 128:(s + 1) * 128],
                           kc == 0, kc == 7, [wgu0B, hTB[s]], [bankB[gub[gi]]])

            ob = {0: oproj(0), 1: oproj(1)}
            pend = None
            for s in range(4):
                if s + 2 < 4:
                    ob[s + 2] = oproj(s + 2)
                pbs = ob.pop(s)
                kx = xs_next()
                P.dma("sp", xs[kx][:, :], x_d[tok0 + s * 128: tok0 + (s + 1) * 128, :], writes=[xsB[kx]], owner=xsB[kx])
                k = nxt("st", 2)
                s_, sB = st[k], stB[k]
                for c2 in range(2):
                    act(junk[:, :], banks[pbs[c2]][:, :], AF.Square, [bankB[pbs[c2]]], [junkB, sB], accum_out=s_[:, c2:c2 + 1])
                dve(lambda e, s_=s_: e.tensor_scalar(out=s_[:, 3:4], in0=s_[:, 0:1], scalar1=s_[:, 1:2], scalar2=1.0 / D,
                                                     op0=ALU.add, op1=ALU.mult), [sB], [sB])
                rstd_from(s_, sB, 3, 4, 5)
                for c2 in range(2):
                    kt_ = nxt("tmp", 2)
                    dve(lambda e, s_=s_, kt_=kt_, c2=c2, pbs=pbs: e.scalar_tensor_tensor(
                        out=tmp[kt_][:, :], in0=banks[pbs[c2]][:, :], scalar=s_[:, 5:6], in1=gpost_mix[:, c2 * 512:(c2 + 1) * 512],
                        op0=ALU.mult, op1=ALU.mult), [bankB[pbs[c2]], sB, gpmB], [tmpB[kt_]])
                    brelease(pbs[c2])
                    pool(lambda e, kt_=kt_, c2=c2, kx=kx: e.tensor_tensor(
                        out=xs[kx][:, c2 * 512:(c2 + 1) * 512], in0=xs[kx][:, c2 * 512:(c2 + 1) * 512], in1=tmp[kt_][:, :],
                        op=ALU.add), [tmpB[kt_], xsB[kx]], [xsB[kx]])
                P.dma("pool", out_d[tok0 + s * 128: tok0 + (s + 1) * 128, :], xs[kx][:, :], reads=[xsB[kx]], writes=[odB[s]],
                      owner=xsB[kx])
                kxn = norm_part(xs[kx][:, :], xsB[kx])
                if pend is not None:
                    transpose_part(pend[1], gffnT, gffnB, hT[:, :, pend[0] * 128:(pend[0] + 1) * 128], hTB[pend[0]])
                    gu0_slice(pend[0])
                pend = (s, kxn)
            transpose_part(pend[1], gffnT, gffnB, hT[:, :, pend[0] * 128:(pend[0] + 1) * 128], hTB[pend[0]])
            gu0_slice(pend[0])
            for q in range(2):
                pgt, pup = gub[2 * q], gub[2 * q + 1]
                ks = nxt("sig", 3)
                act(sig[ks][:, :], banks[pgt][:, :], AF.Silu, [bankB[pgt]], [sigB[ks]])
                dve(lambda e, ks=ks, pup=pup, q=q: e.tensor_tensor(out=hidT[:, q, :], in0=banks[pup][:, :], in1=sig[ks][:, :],
                                                                    op=ALU.mult), [bankB[pup], sigB[ks]], [hidB[q]])
                brelease(pgt)
                brelease(pup)
            ckpt('oproj', [('h2T', hT[:, :, :].rearrange('p a b -> p (a b)'), hTB)], at=(bq, tt))

        def tile_ffn(ti):
            bq, tt = tiles[ti]
            hT, hTB = hT2[ti % 2], hTB2[ti % 2]
            tok0 = bq * SEQ + tt * T
            nxt_new_seq = (ti + 1 < len(tiles)) and tiles[ti + 1][1] == 0
            s1k = {}
            for jj in range(1, 11):
                (wg_, wu_), wB_ = wload("GU%d" % jj)
                for q in range(2):
                    j = 2 * jj + q
                    pgt = proj_fm(wg_, wB_, q * 128, hT, hTB)
                    pup = proj_fm(wu_, wB_, q * 128, hT, hTB)
                    ks = nxt("sig", 3)
                    act(sig[ks][:, :], banks[pgt][:, :], AF.Silu, [bankB[pgt]], [sigB[ks]])
                    dve(lambda e, ks=ks, pup=pup, j=j: e.tensor_tensor(out=hidT[:, j, :], in0=banks[pup][:, :], in1=sig[ks][:, :],
                                                                        op=ALU.mult), [bankB[pup], sigB[ks]], [hidB[j]])
                if ti + 1 < len(tiles) and jj in (1, 3, 5, 7):
                    s1k[(jj - 1) // 2] = stage1_A(ti + 1, (jj - 1) // 2)
                if ti + 1 < len(tiles) and jj in (2, 4, 6, 8):
                    stage1_B(ti + 1, (jj - 2) // 2, s1k[(jj - 2) // 2])
            if nxt_new_seq:
                mem_stage(tiles[ti + 1][0])
            pd = [[None] * 4 for _ in range(2)]
            for c2 in range(2):
                for s in range(4):
                    pd[c2][s] = balloc(hold=True)
                for kg, (k0, kn) in enumerate(KG):
                    (wd_,), wdB_ = wload("D%d_%d" % (c2, kg))
                    for s in range(4):
                        for kl in range(kn):
                            kc = k0 + kl
                            mm(banks[pd[c2][s]][:, :], hidT[:, kc, s * 128:(s + 1) * 128], wd_[:, kl, :], kc == 0, kc == NFF - 1,
                               [wdB_, hidB[kc]], [bankB[pd[c2][s]]], sig=(kl == kn - 1))
            for s in range(4):
                kx = xs_next()
                P.dma("pool", xs[kx][:, :], out_d[tok0 + s * 128: tok0 + (s + 1) * 128, :], reads=[odB[s]], writes=[xsB[kx]],
                      owner=xsB[kx])
                k = nxt("st", 2)
                s_, sB = st[k], stB[k]
                for c2 in range(2):
                    act(junk[:, :], banks[pd[c2][s]][:, :], AF.Square, [bankB[pd[c2][s]]], [junkB, sB], accum_out=s_[:, c2:c2 + 1])
                dve(lambda e, s_=s_: e.tensor_scalar(out=s_[:, 3:4], in0=s_[:, 0:1], scalar1=s_[:, 1:2], scalar2=1.0 / D,
                                                     op0=ALU.add, op1=ALU.mult), [sB], [sB])
                rstd_from(s_, sB, 3, 4, 5)
                for c2 in range(2):
                    kt_ = nxt("tmp", 2)
                    dve(lambda e, s_=s_, kt_=kt_, c2=c2, s=s: e.scalar_tensor_tensor(
                        out=tmp[kt_][:, :], in0=banks[pd[c2][s]][:, :], scalar=s_[:, 5:6], in1=gpost_ffn[:, c2 * 512:(c2 + 1) * 512],
                        op0=ALU.mult, op1=ALU.mult), [bankB[pd[c2][s]], sB, gpfB], [tmpB[kt_]])
                    dve(lambda e, kt_=kt_, c2=c2, kx=kx: e.tensor_tensor(
                        out=xs[kx][:, c2 * 512:(c2 + 1) * 512], in0=xs[kx][:, c2 * 512:(c2 + 1) * 512], in1=tmp[kt_][:, :],
                        op=ALU.add), [tmpB[kt_], xsB[kx]], [xsB[kx]])
                    brelease(pd[c2][s])
                P.dma("pool", out_d[tok0 + s * 128: tok0 + (s + 1) * 128, :], xs[kx][:, :], reads=[xsB[kx]], writes=[odB[s]],
                      owner=xsB[kx])
            ckpt('ffn', [], at=(bq, tt))

        mem_stage(0)
        for s in range(4):
            stage1_B(0, s, stage1_A(0, s))
        for ti in range(len(tiles)):
            tile_front(ti)
            tile_ffn(ti)
    except _Stop:
        pass
    for o_ in dump_ops:
        P._wait('sp', o_.sem, o_.val)
    for b_ in xsB:
        for r in b_.dma_readers:
            P._wait("pool", r.sem, r.val)
    print("instr counts (signals):", P.cnt, "sems:", P.nsem)
    return nc


_NC_CACHE = {}


def kernel(**inputs):
    n = 8
    if "nc" not in _NC_CACHE:
        _NC_CACHE["nc"] = build()
    nc = _NC_CACHE["nc"]
    x = np.ascontiguousarray(inputs["x"], dtype=np.float32)
    mem = np.ascontiguousarray(inputs["mem"], dtype=np.float32)
    shared = {}
    for k in ("ln_mix_pre", "ln_mix_post", "ln_ffn_pre", "ln_ffn_post", "ln_mem", "ln_v_gain", "ln_v_bias"):
        shared[k] = np.ascontiguousarray(inputs[k], dtype=np.float32).reshape(1, -1)
    shared["w_in"] = np.ascontiguousarray(inputs["w_in"][0], dtype=np.float32)
    shared["w_spatial"] = np.ascontiguousarray(inputs["w_spatial"][0], dtype=np.float32)
    shared["b_spatial"] = np.ascontiguousarray(inputs["b_spatial"][0], dtype=np.float32).reshape(1, 768)
    shared["rel_bias"] = np.ascontiguousarray(inputs["rel_bias"], dtype=np.float32)
    for k in ("w_mem_kv", "w_branch_a", "w_branch_b", "w_branch_c", "w_out", "w_ffn_gate", "w_ffn_up", "w_ffn_down"):
        shared[k] = np.ascontiguousarray(inputs[k][0], dtype=np.float32)
    in_maps = []
    for c in range(n):
        m = dict(shared)
        m["x"] = x[2 * c:2 * c + 2].reshape(NSEQ * SEQ, D)
        m["mem"] = mem[2 * c:2 * c + 2].reshape(NSEQ * MEM, D)
        in_maps.append(m)
    res = run_bass_kernel_spmd(nc, in_maps, core_ids=list(range(n)))
    outs = [np.asarray(r["out"]).reshape(NSEQ, SEQ, D) for r in res.results]
    return np.concatenate(outs, axis=0).astype(np.float32, copy=False)
```
